# Optimizing a Trainium2 kernel written in Bass

```python
import jax
import jax.numpy as jnp
from jax import lax
import numpy as np

D_MODEL = 2048
BATCH = 4
SEQ = 2048
DEPTH = 4

CTX_LEN = 256
GRID_W = 64
MIX_W = D_MODEL
N_BRANCH = 3
NORM_EPS = 1e-6
RWKV_HEAD_DIM = 64
RWKV_HEADS = MIX_W // RWKV_HEAD_DIM
DECAY_LORA = 96
AICL_LORA = 96
SHIFT_K = 3
RWKV_GN_EPS = 64e-5
N_SHIFT = 3 * MIX_W + 2 * DECAY_LORA + 2 * AICL_LORA
N_A = N_SHIFT + MIX_W
HGRN_EXPAND = 128
HGRN_HEADS = MIX_W // HGRN_EXPAND
HGRN_HEAD_DIM = MIX_W // HGRN_HEADS
HGRN_CHUNK = 64
HGRN_NORM_EPS = 1e-5
N_B = 5 * MIX_W
NA_HEADS = 16
NA_HEAD_DIM = MIX_W // NA_HEADS
NA_KH = 8
NA_KW = 16
NA_QBLK = 16
NA_KBLK = 32
N_C = 4 * MIX_W
N_GATE = N_BRANCH * D_MODEL
N_IN = N_A + N_B + N_C + N_GATE

kernel_name = 'hybrid_rwkv7_hgrn2_natten_dit_block'


def rmsnorm(x, g, eps=NORM_EPS):
    xf = x.astype(jnp.float32)
    y = xf * lax.rsqrt(jnp.mean(xf * xf, axis=-1, keepdims=True) + eps)
    return (y * g.astype(jnp.float32)).astype(x.dtype)


def heads(t, n_heads):
    return t.reshape(t.shape[:-1] + (n_heads, t.shape[-1] // n_heads))


def centred_dwconv(u, w):
    pad = w.shape[0] // 2
    return lax.conv_general_dilated(u, w[:, None, :].astype(u.dtype), window_strides=(1,),
                                    padding=[(pad, pad)], dimension_numbers=('NWC', 'WIO', 'NWC'),
                                    feature_group_count=u.shape[-1])


def run_direction(scan_fn, arrs, s0, reverse, readout):
    if reverse:
        arrs = tuple(jnp.flip(a, axis=1) for a in arrs)
    o, s = scan_fn(*arrs, s0, readout)
    if reverse and readout:
        o = jnp.flip(o, axis=1)
    return o, s


def rwkv7_scan(r, w, k, v, kk, b, s0, readout):
    def step(s, inp):
        r_t, w_t, k_t, v_t, kk_t, b_t = inp
        sa = jnp.einsum('bhvk,bhk->bhv', s, kk_t)
        s = s * w_t[:, :, None, :] - sa[..., None] * b_t[:, :, None, :] + v_t[..., None] * k_t[:, :, None, :]
        o = jnp.einsum('bhvk,bhk->bhv', s, r_t) if readout else None
        return s, o
    xs = tuple(jnp.moveaxis(t, 1, 0) for t in (r, w, k, v, kk, b))
    s, o = lax.scan(step, s0, xs)
    return (jnp.moveaxis(o, 0, 1) if readout else None), s


def rwkv_prep(p, shift_w, w0, w_up, a0, a_up, k_k, k_a):
    f32 = jnp.float32
    u = centred_dwconv(p[..., :N_SHIFT], shift_w)
    gate = p[..., N_SHIFT:]
    r, k, v = (u[..., i * MIX_W:(i + 1) * MIX_W] for i in range(3))
    lo = u[..., 3 * MIX_W:]
    wd = lo[..., :2 * DECAY_LORA].reshape(lo.shape[:-1] + (2, DECAY_LORA))
    ad = lo[..., 2 * DECAY_LORA:].reshape(lo.shape[:-1] + (2, AICL_LORA))
    wl = (w0 + jnp.einsum('bnjr,jrc->bnjc', jnp.tanh(wd), w_up)).astype(f32)
    decay = jnp.exp(-jnp.exp(-jax.nn.softplus(-wl) - 0.5))
    a = jax.nn.sigmoid((a0 + jnp.einsum('bnjr,jrc->bnjc', ad, a_up)).astype(f32))
    kk = heads((k * k_k).astype(f32), RWKV_HEADS)
    kk = kk / jnp.maximum(jnp.sqrt(jnp.sum(kk * kk, axis=-1, keepdims=True)), 1e-12)
    kkw = kk.reshape(kk.shape[:-2] + (MIX_W,))
    kf = k.astype(f32)
    dirs = []
    for j in range(2):
        k_j = kf * (1.0 + (a[:, :, j] - 1.0) * k_a)
        b_j = kkw * a[:, :, j]
        dirs.append((heads(decay[:, :, j], RWKV_HEADS), heads(k_j, RWKV_HEADS), heads(b_j, RWKV_HEADS)))
    return heads(r.astype(f32), RWKV_HEADS), heads(v.astype(f32), RWKV_HEADS), kk, dirs, gate


def rwkv_post(o, r, dirs, v, gate, r_k, ln_g, ln_b):
    mu = jnp.mean(o, axis=-1, keepdims=True)
    var = jnp.mean(jnp.square(o - mu), axis=-1, keepdims=True)
    on = ((o - mu) * lax.rsqrt(var + RWKV_GN_EPS)).reshape(o.shape[:-2] + (MIX_W,)) * ln_g + ln_b
    bonus = sum(jnp.sum(r * k_j * r_k, axis=-1, keepdims=True) * v for (_, k_j, _) in dirs)
    y = (on + bonus.reshape(on.shape)) * jax.nn.silu(gate.astype(jnp.float32))
    return y.astype(gate.dtype)


def rwkv_branch(p_ctx, p_lat, shift_w, w0, w_up, a0, a_up, k_k, k_a, r_k, ln_g, ln_b, ctx_out):
    prm = (shift_w, w0, w_up, a0, a_up, k_k, k_a)
    rc, vc, kkc, dirs_c, gate_c = rwkv_prep(p_ctx, *prm)
    rl, vl, kkl, dirs_l, gate_l = rwkv_prep(p_lat, *prm)
    s0 = jnp.zeros((p_lat.shape[0], RWKV_HEADS, RWKV_HEAD_DIM, RWKV_HEAD_DIM), jnp.float32)
    outs_c, outs_l = [], []
    for j, rev in enumerate((False, True)):
        wc, kc, bc = dirs_c[j]
        wl, kl, bl = dirs_l[j]
        oc, sc = run_direction(rwkv7_scan, (rc, wc, kc, vc, kkc, bc), s0, rev, ctx_out)
        ol, _ = run_direction(rwkv7_scan, (rl, wl, kl, vl, kkl, bl), sc, rev, True)
        outs_c.append(oc)
        outs_l.append(ol)
    y_lat = rwkv_post(outs_l[0] + outs_l[1], rl, dirs_l, vl, gate_l, r_k, ln_g, ln_b)
    y_ctx = rwkv_post(outs_c[0] + outs_c[1], rc, dirs_c, vc, gate_c, r_k, ln_g, ln_b) if ctx_out else None
    return y_ctx, y_lat


def hgrn_lower_bounds(logits):
    p = jax.nn.softmax(logits.astype(jnp.float32), axis=0)
    return jnp.cumsum(p, axis=0) - p[0:1]


def gla_chunk_scan(q, k, v, g, s0, readout):
    bsz, n, h, _ = q.shape
    nc = n // HGRN_CHUNK

    def chunks(t):
        return t.reshape(bsz, nc, HGRN_CHUNK, h, t.shape[-1]).transpose(1, 0, 3, 2, 4)

    lower = jnp.tril(jnp.ones((HGRN_CHUNK, HGRN_CHUNK), dtype=bool))

    def step(s, inp):
        qc, kc, vc, gc = inp
        bcum = jnp.cumsum(gc, axis=2)
        blast = bcum[:, :, -1]
        s_new = jnp.exp(blast)[..., None] * s + jnp.einsum('bhjd,bhje->bhde', kc * jnp.exp(blast[:, :, None] - bcum), vc)
        if not readout:
            return s_new, None
        o_inter = jnp.einsum('bhid,bhde->bhie', qc * jnp.exp(bcum), s)
        decay = jnp.exp(jnp.where(lower[:, :, None], bcum[:, :, :, None, :] - bcum[:, :, None, :, :], -jnp.inf))
        att = jnp.einsum('bhid,bhjd,bhijd->bhij', qc, kc, decay)
        return s_new, o_inter + jnp.einsum('bhij,bhje->bhie', att, vc)

    s, o = lax.scan(step, s0, tuple(chunks(t) for t in (q, k, v, g)))
    if readout:
        o = o.transpose(1, 0, 3, 2, 4).reshape(bsz, n, h, o.shape[-1])
    return o, s


def hgrn_prep(p, lb):
    f32 = jnp.float32
    q, f_fwd, f_bwd, i, gate = jnp.split(p, 5, axis=-1)
    q = heads(jax.nn.silu(q.astype(f32)), HGRN_HEADS)
    v = heads(i.astype(f32), HGRN_HEADS)
    dirs = []
    for fr in (f_fwd, f_bwd):
        f = lb + (1.0 - lb) * jax.nn.sigmoid(fr.astype(f32))
        dirs.append((heads(1.0 - f, HGRN_HEADS), heads(jnp.log(f), HGRN_HEADS)))
    return q, v, dirs, gate


def hgrn_post(o, gate, norm_g):
    on = o * lax.rsqrt(jnp.mean(o * o, axis=-1, keepdims=True) + HGRN_NORM_EPS) * norm_g
    y = on.reshape(o.shape[:-2] + (MIX_W,)) * jax.nn.silu(gate.astype(jnp.float32))
    return y.astype(gate.dtype)


def hgrn_branch(p_ctx, p_lat, lb, norm_g, ctx_out):
    qc, vc, dirs_c, gate_c = hgrn_prep(p_ctx, lb)
    ql, vl, dirs_l, gate_l = hgrn_prep(p_lat, lb)
    s0 = jnp.zeros((p_lat.shape[0], HGRN_HEADS, HGRN_HEAD_DIM, HGRN_HEAD_DIM), jnp.float32)
    outs_c, outs_l = [], []
    for j, rev in enumerate((False, True)):
        kc, gc = dirs_c[j]
        kl, gl = dirs_l[j]
        oc, sc = run_direction(gla_chunk_scan, (qc, kc, vc, gc), s0, rev, ctx_out)
        ol, _ = run_direction(gla_chunk_scan, (ql, kl, vl, gl), sc, rev, True)
        outs_c.append(oc)
        outs_l.append(ol)
    y_lat = hgrn_post(outs_l[0] + outs_l[1], gate_l, norm_g)
    y_ctx = hgrn_post(outs_c[0] + outs_c[1], gate_c, norm_g) if ctx_out else None
    return y_ctx, y_lat


def na_col_tables():
    n_cb = GRID_W // NA_QBLK
    col = np.arange(GRID_W)
    cs = np.clip(col - NA_KW // 2, 0, GRID_W - NA_KW)
    kb = np.minimum(cs[::NA_QBLK], GRID_W - NA_KBLK)
    col_key = kb[:, None] + np.arange(NA_KBLK)
    cq = col.reshape(n_cb, NA_QBLK)
    csq = cs.reshape(n_cb, NA_QBLK)
    valid = (col_key[:, None, :] >= csq[..., None]) & (col_key[:, None, :] < csq[..., None] + NA_KW)
    off = np.clip(col_key[:, None, :] - cq[..., None] + NA_KW - 1, 0, 2 * NA_KW - 2)
    return col_key, valid, off


def na_branch(p_ctx, p_lat, rpb, ctx_out):
    f32 = jnp.float32
    scale = NA_HEAD_DIM ** -0.5

    def bhnd(t):
        return jnp.moveaxis(heads(t, NA_HEADS), 2, 1)

    q_c, k_c, v_c, gate_c = jnp.split(p_ctx, 4, axis=-1)
    q_c, k_c, v_c = bhnd(q_c), bhnd(k_c), bhnd(v_c)
    y_ctx = None
    if ctx_out:
        s = (jnp.einsum('bhqd,bhkd->bhqk', q_c, k_c) * scale).astype(f32)
        o = jnp.einsum('bhqk,bhkd->bhqd', jax.nn.softmax(s, axis=-1).astype(v_c.dtype), v_c)
        y_ctx = jnp.moveaxis(o, 1, 2).reshape(p_ctx.shape[:-1] + (MIX_W,)) * jax.nn.silu(gate_c)

    q_l, k_l, v_l, gate_l = jnp.split(p_lat, 4, axis=-1)
    bsz, L = p_lat.shape[:2]
    rows = L // GRID_W
    kh = min(NA_KH, rows)
    n_cb = GRID_W // NA_QBLK

    def grid(t):
        return bhnd(t).reshape(bsz, NA_HEADS, rows, GRID_W, NA_HEAD_DIM)

    qg, kg, vg = grid(q_l), grid(k_l), grid(v_l)
    col_key, col_valid, col_off = na_col_tables()

    def row_block(r):
        rs = jnp.clip(r - kh // 2, 0, rows - kh)
        q_blk = lax.dynamic_index_in_dim(qg, r, axis=2, keepdims=False).reshape(bsz, NA_HEADS, n_cb, NA_QBLK, NA_HEAD_DIM)
        k_blk = lax.dynamic_slice_in_dim(kg, rs, kh, axis=2)[:, :, :, col_key]
        v_blk = lax.dynamic_slice_in_dim(vg, rs, kh, axis=2)[:, :, :, col_key]
        dr = rs + jnp.arange(kh) - r + (NA_KH - 1)
        bias = jnp.take(rpb, dr, axis=1)[:, :, col_off].transpose(0, 2, 3, 1, 4)
        s_loc = (jnp.einsum('bhmqd,bhimld->bhmqil', q_blk, k_blk) * scale).astype(f32) + bias[None].astype(f32)
        s_loc = jnp.where(col_valid[:, :, None, :], s_loc, -jnp.inf)
        s_ctx = (jnp.einsum('bhmqd,bhnd->bhmqn', q_blk, k_c) * scale).astype(f32)
        s = jnp.concatenate([s_loc.reshape(bsz, NA_HEADS, n_cb, NA_QBLK, kh * NA_KBLK), s_ctx], axis=-1)
        p = jax.nn.softmax(s, axis=-1).astype(v_l.dtype)
        p_loc = p[..., :kh * NA_KBLK].reshape(bsz, NA_HEADS, n_cb, NA_QBLK, kh, NA_KBLK)
        o = jnp.einsum('bhmqil,bhimld->bhmqd', p_loc, v_blk) + jnp.einsum('bhmqn,bhnd->bhmqd', p[..., kh * NA_KBLK:], v_c)
        return o.reshape(bsz, NA_HEADS, GRID_W, NA_HEAD_DIM)

    o = lax.map(row_block, jnp.arange(rows))
    o = o.transpose(1, 0, 3, 2, 4).reshape(bsz, L, MIX_W)
    return y_ctx, o * jax.nn.silu(gate_l)


def merge(ya, yb, yc, gate_logits, w_branch, w_out):
    ga, gb, gc = jnp.split(jax.nn.sigmoid(gate_logits), 3, axis=-1)
    m = ga * (ya @ w_branch[0]) + gb * (yb @ w_branch[1]) + gc * (yc @ w_branch[2])
    return m @ w_out


def layer(x, xc, c, c_ctx, w_ada, b_ada, norm_g, w_in, rwkv_shift, rwkv_w0, rwkv_w_up, rwkv_a0, rwkv_a_up,
          rwkv_k_k, rwkv_k_a, rwkv_r_k, rwkv_ln_g, rwkv_ln_b, lb, hgrn_norm_g, na_rpb, w_branch, w_out, ctx_out):
    shift, scale, gate = jnp.split(jax.nn.silu(c) @ w_ada + b_ada, 3, axis=-1)
    shift_c, scale_c, gate_c = jnp.split(jax.nn.silu(c_ctx) @ w_ada + b_ada, 3, axis=-1)
    h = rmsnorm(x, norm_g) * (1.0 + scale[:, None]) + shift[:, None]
    hc = rmsnorm(xc, norm_g) * (1.0 + scale_c) + shift_c
    cuts = [N_A, N_A + N_B, N_A + N_B + N_C]
    pa, pb, pn, pg = jnp.split(h @ w_in, cuts, axis=-1)
    pa_c, pb_c, pn_c, pg_c = jnp.split(hc @ w_in, cuts, axis=-1)
    ya_c, ya = rwkv_branch(pa_c, pa, rwkv_shift, rwkv_w0, rwkv_w_up, rwkv_a0, rwkv_a_up, rwkv_k_k, rwkv_k_a,
                           rwkv_r_k, rwkv_ln_g, rwkv_ln_b, ctx_out)
    yb_c, yb = hgrn_branch(pb_c, pb, lb, hgrn_norm_g, ctx_out)
    yc_c, yc = na_branch(pn_c, pn, na_rpb, ctx_out)
    x = x + gate[:, None] * merge(ya, yb, yc, pg, w_branch, w_out)
    xc = xc + gate_c * merge(ya_c, yb_c, yc_c, pg_c, w_branch, w_out) if ctx_out else None
    return x, xc


def setup_inputs(seed: int = 0) -> dict:
    key = jax.random.key(seed)
    ks = jax.random.split(key, 24)
    f32 = jnp.float32

    def nrm(k, shape, s):
        return s * jax.random.normal(k, shape, f32)

    return {
        'x': nrm(ks[0], (BATCH, SEQ, D_MODEL), 1.0),
        'c': nrm(ks[1], (BATCH, D_MODEL), 1.0),
        'ctx': nrm(ks[2], (BATCH, CTX_LEN, D_MODEL), 1.0),
        'c_ctx': nrm(ks[3], (D_MODEL,), 1.0),
        'w_ada': nrm(ks[4], (DEPTH, D_MODEL, 3 * D_MODEL), 0.5 * D_MODEL ** -0.5),
        'b_ada': nrm(ks[5], (DEPTH, 3 * D_MODEL), 0.02),
        'norm_g': 1.0 + nrm(ks[6], (DEPTH, D_MODEL), 0.02),
        'w_in': nrm(ks[7], (DEPTH, D_MODEL, N_IN), D_MODEL ** -0.5),
        'rwkv_shift': jnp.array([0.25, 0.5, 0.25], f32)[None, :, None] + nrm(ks[8], (DEPTH, SHIFT_K, N_SHIFT), 0.05),
        'rwkv_w0': nrm(ks[9], (DEPTH, 2, MIX_W), 1.5) - 1.0,
        'rwkv_w_up': nrm(ks[10], (DEPTH, 2, DECAY_LORA, MIX_W), 0.5 * DECAY_LORA ** -0.5),
        'rwkv_a0': nrm(ks[11], (DEPTH, 2, MIX_W), 0.5),
        'rwkv_a_up': nrm(ks[12], (DEPTH, 2, AICL_LORA, MIX_W), 0.5 * AICL_LORA ** -0.5),
        'rwkv_k_k': 0.85 + nrm(ks[13], (DEPTH, MIX_W), 0.05),
        'rwkv_k_a': 1.0 + nrm(ks[14], (DEPTH, MIX_W), 0.05),
        'rwkv_r_k': nrm(ks[15], (DEPTH, RWKV_HEADS, RWKV_HEAD_DIM), 0.1),
        'rwkv_ln_g': 1.0 + nrm(ks[16], (DEPTH, MIX_W), 0.02),
        'rwkv_ln_b': nrm(ks[17], (DEPTH, MIX_W), 0.02),
        'hgrn_lb_logits': 1.0 + nrm(ks[18], (DEPTH, MIX_W), 0.1),
        'hgrn_norm_g': 1.0 + nrm(ks[19], (DEPTH, HGRN_HEAD_DIM), 0.02),
        'na_rpb': nrm(ks[20], (DEPTH, NA_HEADS, 2 * NA_KH - 1, 2 * NA_KW - 1), 0.1),
        'w_branch': nrm(ks[21], (DEPTH, N_BRANCH, MIX_W, D_MODEL), MIX_W ** -0.5),
        'w_out': nrm(ks[22], (DEPTH, D_MODEL, D_MODEL), D_MODEL ** -0.5),
        'final_g': 1.0 + nrm(ks[23], (D_MODEL,), 0.02),
    }


def reference(x, c, ctx, c_ctx, w_ada, b_ada, norm_g, w_in, rwkv_shift, rwkv_w0, rwkv_w_up, rwkv_a0, rwkv_a_up,
              rwkv_k_k, rwkv_k_a, rwkv_r_k, rwkv_ln_g, rwkv_ln_b, hgrn_lb_logits, hgrn_norm_g, na_rpb,
              w_branch, w_out, final_g):
    lbs = hgrn_lower_bounds(hgrn_lb_logits)
    xc = ctx
    for l in range(DEPTH):
        x, xc = layer(x, xc, c, c_ctx, w_ada[l], b_ada[l], norm_g[l], w_in[l], rwkv_shift[l], rwkv_w0[l],
                      rwkv_w_up[l], rwkv_a0[l], rwkv_a_up[l], rwkv_k_k[l], rwkv_k_a[l], rwkv_r_k[l],
                      rwkv_ln_g[l], rwkv_ln_b[l], lbs[l], hgrn_norm_g[l], na_rpb[l], w_branch[l], w_out[l],
                      ctx_out=(l < DEPTH - 1))
    return rmsnorm(x, final_g)
```

```python
import contextlib
import numpy as np
import concourse.bass as bass
import concourse.mybir as mybir
from concourse.bass_utils import run_bass_kernel_spmd

F32 = mybir.dt.float32
BF16 = mybir.dt.bfloat16
ALU = mybir.AluOpType
AF = mybir.ActivationFunctionType
AX = mybir.AxisListType

D = 2048
KC = 16
B = 4
SEQ = 2048
CTX = 256
NT = CTX + SEQ
DEPTH = 4
N_SHIFT = 6528
N_A = N_SHIFT + D
N_B = 5 * D
N_C = 4 * D
N_IN = N_A + N_B + N_C + 3 * D
TBLK = [(0, 256), (256, 512), (768, 512), (1280, 512), (1792, 512)]

DEBUG = {}
STOP_AFTER = None


class Tl:
    def __init__(self, handle, name):
        self.h = handle
        self.name = name
        self.w = None
        self.r = {}
        self.is_dram = False

    def __getitem__(self, k):
        return V(self, self.h[k])

    @property
    def a(self):
        return V(self, self.h if self.is_dram else self.h[:])


class V:
    def __init__(self, tile, ap):
        self.tile = tile
        self.ap = ap

    def __getitem__(self, k):
        return V(self.tile, self.ap[k])

    def r(self, s, **kw):
        return V(self.tile, self.ap.rearrange(s, **kw))

    def bc(self, shape):
        return V(self.tile, self.ap.broadcast_to(shape))

    def us(self, ax):
        return V(self.tile, self.ap.unsqueeze(ax))

    def bitcast(self, dt):
        return V(self.tile, self.ap.bitcast(dt))

    @property
    def shape(self):
        return self.ap.shape


ENG = ["pe", "act", "dve", "pool", "sp"]
NSLOT = {"sp": 12, "pool": 12, "act": 6}
ARENA = 212800


class KB:
    def __init__(self):
        self.nc = bass.Bass("TRN2", target_bir_lowering=False)
        self.es = contextlib.ExitStack()
        self.streams = {e: [] for e in ENG}
        self.cnt = {e: 0 for e in ENG}
        self.waited = {e: {} for e in ENG}
        self.sems = {}
        for e in ENG:
            self.sems[e] = self.es.enter_context(self.nc.semaphore("s_" + e))
        self.slot_use = {}
        self.slot_next = {q: 0 for q in NSLOT}
        for q, n in NSLOT.items():
            for i in range(n):
                k = "d_%s_%d" % (q, i)
                self.sems[k] = self.es.enter_context(self.nc.semaphore(k))
                self.slot_use[k] = 0
        self.n_instr = 0
        self.arena = None
        self.sp_ = 0
        self.sp_max = 0

    def sb(self, name, shape, dt):
        if self.arena is None:
            self.arena = self.es.enter_context(self.nc.sbuf_tensor("arena", [128, ARENA], mybir.dt.uint8))
            self.sp_ = 0
        esz = 2 if dt == BF16 else 4
        n = 1
        for d_ in shape[1:]:
            n *= d_
        nb = (n * esz + 63) // 64 * 64
        assert self.sp_ + nb <= ARENA, "SBUF arena overflow at %s: %d + %d" % (name, self.sp_, nb)
        ap = self.arena[0:shape[0], self.sp_:self.sp_ + n * esz].bitcast(dt)
        self.sp_ += nb
        self.sp_max = max(self.sp_max, self.sp_)
        if len(shape) == 3:
            ap = ap.rearrange("p (a b) -> p a b", b=shape[2])
        elif len(shape) == 4:
            ap = ap.rearrange("p (a b c) -> p a b c", b=shape[2], c=shape[3])
        t = Tl(ap, name)
        t.is_dram = True
        return t

    @contextlib.contextmanager
    def scope(self):
        mark = self.sp_
        yield
        self.barrier()
        self.sp_ = mark

    def ps(self, name, shape, dt):
        h = self.es.enter_context(self.nc.psum_tensor(name, list(shape), dt))
        return Tl(h, name)

    def dram(self, name, shape, dt, kind):
        h = self.nc.dram_tensor(name, list(shape), dt, kind=kind)
        t = Tl(h.ap(), name)
        t.is_dram = True
        return t

    def _deps(self, reads, writes):
        deps = {}

        def add(k, v):
            if deps.get(k, 0) < v:
                deps[k] = v
        for t in reads:
            if t.w:
                add(*t.w)
        for t in writes:
            if t.w:
                add(*t.w)
            for k, v in t.r.items():
                add(k, v)
        return deps

    def _waits(self, eng, deps):
        waits = []
        wd = self.waited[eng]
        for k, v in deps.items():
            if k == eng:
                if eng != "pe" and v == self.cnt[eng]:
                    waits.append((k, v))
                continue
            if wd.get(k, 0) < v:
                wd[k] = v
                waits.append((k, v))
        return waits

    def _commit(self, key, val, reads, writes):
        for t in reads:
            if t.r.get(key, 0) < val:
                t.r[key] = val
        for t in writes:
            t.w = (key, val)
            t.r = {}

    def emit(self, eng, fn, reads, writes):
        deps = self._deps(reads, writes)
        waits = self._waits(eng, deps)
        self.cnt[eng] += 1
        self.streams[eng].append((waits, fn, (eng, 1)))
        self._commit(eng, self.cnt[eng], reads, writes)
        self.n_instr += 1

    def I(self, eng, method, *args, **kw):
        reads, writes = [], []

        def conv(x, is_out):
            if isinstance(x, V):
                (writes if is_out else reads).append(x.tile)
                return x.ap
            return x
        a2 = [conv(a, i == 0) for i, a in enumerate(args)]
        k2 = {k: conv(v, k in ("out", "accum_out")) for k, v in kw.items()}
        self.emit(eng, lambda e: getattr(e, method)(*a2, **k2), reads, writes)

    def dma(self, q, out, in_, **kw):
        reads = [in_.tile]
        writes = [out.tile]
        n = NSLOT[q]
        slot = "d_%s_%d" % (q, self.slot_next[q] % n)
        self.slot_next[q] += 1
        deps = self._deps(reads, writes)
        prev = self.slot_use[slot]
        if prev:
            deps[slot] = max(deps.get(slot, 0), prev)
        waits = self._waits(q, deps)
        self.slot_use[slot] = prev + 16
        oa, ia = out.ap, in_.ap
        self.streams[q].append((waits, lambda e: e.dma_start(out=oa, in_=ia, **kw), (slot, 16)))
        self._commit(slot, prev + 16, reads, writes)
        self.n_instr += 1

    def barrier(self):
        tgt = {e: self.cnt[e] for e in ENG}
        for k, v in self.slot_use.items():
            tgt[k] = v
        for e in ENG:
            waits = []
            for k, v in tgt.items():
                if k == e or v == 0:
                    continue
                if self.waited[e].get(k, 0) < v:
                    self.waited[e][k] = v
                    waits.append((k, v))
            if waits:
                self.streams[e].append((waits, None, None))

    def finish(self):
        self.barrier()
        nc = self.nc
        sems = self.sems
        streams = self.streams

        def replay(name):
            def f(e):
                for waits, fn, inc in streams[name]:
                    for k, v in waits:
                        e.wait_ge(sems[k], v)
                    if fn is not None:
                        ins = fn(e)
                        ins.then_inc(sems[inc[0]], inc[1])
            return f
        with nc.Block() as block:
            block.tensor(replay("pe"))
            block.scalar(replay("act"))
            block.vector(replay("dve"))
            block.gpsimd(replay("pool"))
            block.sync(replay("sp"))
        self.es.close()
        return nc


def pc(v):
    return np.ascontiguousarray(np.asarray(v, np.float32).reshape(-1, 128).T)


NCONST = 12


def make_consts():
    c = np.zeros((NCONST, 128, 128), np.float32)
    i = np.arange(128)
    s, t = np.meshgrid(i, i, indexing="ij")
    same = (s // 64) == (t // 64)
    c[0] = np.eye(128)
    c[1] = (same & (s <= t))
    c[2] = (same & (s >= t))
    c[3] = (same & (s < t))
    c[4] = (same & (s > t))
    c[5] = same
    ssub = (s // 32) == (t // 32)
    c[6] = ssub & (s <= t)
    c[7] = ssub & (s >= t)
    c[8] = same & ((s // 32) % 2 == 0) & ((t // 32) % 2 == 1)
    c[9] = same & ((s // 32) % 2 == 1) & ((t // 32) % 2 == 0)
    c[10][:, :64] = (s[:, :64] % 64) <= t[:, :64]
    c[10][:, 64:] = (s[:, :64] % 64) >= t[:, :64]
    return c


def build(dbg=None, NL=1):
    dbg = dbg or {}
    k = KB()
    nc = k.nc
    xt_in = k.dram("xt", [D, NT], F32, "ExternalInput")
    ct_in = k.dram("ct", [128, KC, 2], F32, "ExternalInput")
    wada = k.dram("wada", [NL, 12, 128, KC, 512], F32, "ExternalInput")
    bada = k.dram("bada", [NL, 128, 48], F32, "ExternalInput")
    ng = k.dram("ng", [NL, 128, KC], F32, "ExternalInput")
    win = k.dram("win", [NL, N_IN // 128, 128, KC, 128], F32, "ExternalInput")
    consts = k.dram("consts", [NCONST, 128, 128], F32, "ExternalInput")
    lbl = k.dram("lbl", [128, KC, DEPTH], F32, "ExternalInput")
    lmask = k.dram("lmask", [128, KC, DEPTH], F32, "ExternalInput")
    hng = k.dram("hng", [NL, 128, 1], F32, "ExternalInput")
    out_t = k.dram("out", [D, SEQ], F32, "ExternalOutput")
    XT = k.dram("xo", [D, NT], F32, "ExternalOutput")
    Y = k.dram("Ys", [3, D, NT], BF16, "ExternalOutput" if "Y" in dbg else "Internal")
    dbg_t = {n: k.dram("dbg_" + n, list(s), d, "ExternalOutput") for n, (s, d) in ((a_, b_) for a_, b_ in dbg.items() if a_ not in ("Y", "_test"))}

    ones_f = k.sb("ones_f", [128, 128], F32)
    k.I("dve", "memset", ones_f[:], 1.0)
    cst = k.sb("cst", [128, NCONST, 128], F32)
    k.dma("sp", cst[:], consts.a.r("c p n -> p c n"))
    ident = cst[:, 0, :]
    eps_t = k.sb("eps_t", [128, 2], F32)
    k.I("dve", "memset", eps_t[:, 0:1], 1e-6)
    k.I("dve", "memset", eps_t[:, 1:2], 1e-5)

    hT = k.sb("hT", [128, KC, NT], BF16)
    sct = k.sb("sct", [128, KC, 2], F32)
    mods = k.sb("mods", [128, 48, 2], F32)
    sc1 = k.sb("sc1", [128, KC, 2], F32)
    bada_s = k.sb("bada_s", [128, 48], F32)
    ng_s = k.sb("ng_s", [128, KC], F32)
    wbuf = [k.sb("wb%d" % i, [128, KC, 128], BF16) for i in range(2)]
    psb = [k.ps("ps%d" % i, [128, 512], F32) for i in range(8)]
    st = {"ps": 0, "wb": 0}

    def next_ps():
        st["ps"] += 1
        return psb[st["ps"] % 8]

    def xt_view(t):
        return t.a.r("(kc p) n -> p kc n", p=128)

    lb_s = k.sb("lb_s", [128, KC, 1], F32)
    oml_s = k.sb("oml_s", [128, KC, 1], F32)
    with k.scope():
        le = k.sb("le", [128, KC, DEPTH], F32)
        lm = k.sb("lm", [128, KC], F32)
        k.dma("sp", le[:], lbl.a)
        k.I("dve", "tensor_reduce", lm[:], le[:], AX.X, ALU.max)
        k.I("dve", "tensor_tensor", le[:], le[:], lm[:].us(2).bc([128, KC, DEPTH]), ALU.subtract)
        k.I("act", "activation", le[:], le[:], AF.Exp)
        k.I("dve", "tensor_reduce", lm[:], le[:], AX.X, ALU.add)
        k.I("dve", "reciprocal", lm[:], lm[:])
        k.I("dve", "tensor_tensor", le[:], le[:], lm[:].us(2).bc([128, KC, DEPTH]), ALU.mult)
        lmk = k.sb("lmk", [128, KC, DEPTH], F32)
        k.dma("sp", lmk[:], lmask.a)
        k.I("dve", "tensor_tensor", le[:], le[:], lmk[:], ALU.mult)
        k.I("dve", "tensor_reduce", lb_s[:, :, 0], le[:], AX.X, ALU.add)
        k.I("dve", "tensor_scalar", oml_s[:], lb_s[:], -1.0, 1.0, ALU.mult, ALU.add)

    k.dma("sp", sct[:], ct_in.a)
    k.I("act", "activation", sct[:], sct[:], AF.Silu)

    def layer_pre(l, first):
        src = xt_in if first else XT
        k.dma("sp", bada_s[:], bada.a[l])
        k.dma("sp", ng_s[:], ng.a[l])
        with k.scope():
            wa_buf = [k.sb("wa%d" % i, [128, KC, 512], F32) for i in range(2)]
            pa = next_ps()
            for t12 in range(12):
                wt = wa_buf[t12 % 2]
                k.dma("sp", wt[:], wada.a[l, t12])
                for j4 in range(4):
                    j = t12 * 4 + j4
                    for kc in range(KC):
                        k.I("pe", "matmul", pa[:, 2 * j:2 * j + 2], wt[:, kc, j4 * 128:(j4 + 1) * 128], sct[:, kc, :],
                            start=(kc == 0), stop=(kc == KC - 1))
            k.I("dve", "tensor_tensor", mods[:], pa[:, 0:96].r("p (j c) -> p j c", c=2),
                bada_s[:].us(2).bc([128, 48, 2]), ALU.add)
            k.I("dve", "scalar_tensor_tensor", sc1[:], mods[:, 16:32, :], 1.0, ng_s[:].us(2).bc([128, KC, 2]),
                ALU.add, ALU.mult)
        with k.scope():
            x_ = k.sb("xb", [128, KC, 512], F32)
            sqb = k.sb("sqb", [128, KC, 512], F32)
            rstd = k.sb("rstd", [128, 512], F32)
            tmpb = k.sb("tmpb", [128, 512], F32)
            for (t0, tn) in TBLK:
                col = 1 if t0 == 0 else 0
                k.dma("sp", x_[:, :, 0:tn], xt_view(src)[:, :, t0:t0 + tn])
                k.I("act", "activation", sqb[:, :, 0:tn], x_[:, :, 0:tn], AF.Square)
                pss = next_ps()
                for kc in range(KC):
                    k.I("pe", "matmul", pss[:, 0:tn], ones_f[:], sqb[:, kc, 0:tn], start=(kc == 0), stop=(kc == KC - 1))
                k.I("act", "activation", rstd[:, 0:tn], pss[:, 0:tn], AF.Sqrt, bias=eps_t[:, 0:1], scale=1.0 / D)
                k.I("dve", "reciprocal", rstd[:, 0:tn], rstd[:, 0:tn])
                for kc in range(KC):
                    k.I("dve", "scalar_tensor_tensor", tmpb[:, 0:tn], x_[:, kc, 0:tn], sc1[:, kc, col:col + 1],
                        rstd[:, 0:tn], ALU.mult, ALU.mult)
                    k.I("act", "activation", hT[:, kc, t0:t0 + tn], tmpb[:, 0:tn], AF.Identity,
                        bias=mods[:, kc, col:col + 1], scale=1.0)
                if first:
                    k.dma("sp", xt_view(XT)[:, :, t0:t0 + tn], x_[:, :, 0:tn])

    def load_w(l, ct, ncols=128):
        wt = wbuf[st["wb"] % len(wbuf)]
        st["wb"] += 1
        k.dma("pool", wt[:, :, 0:ncols], win.a[l, ct][:, :, 0:ncols])
        return wt

    def proj_tile(l, ct, evac, ncols=128):
        wt = load_w(l, ct, ncols)
        for (t0, tn) in TBLK:
            p_ = next_ps()
            for kc in range(KC):
                k.I("pe", "matmul", p_[0:ncols, 0:tn], wt[:, kc, 0:ncols], hT[:, kc, t0:t0 + tn],
                    start=(kc == 0), stop=(kc == KC - 1))
            evac(p_[0:ncols, 0:tn], t0, tn)

    def proj_tok(l, ct, evac):
        wt = load_w(l, ct)
        for tt in range(NT // 128):
            p_ = next_ps()
            for kc in range(KC):
                k.I("pe", "matmul", p_[:, 0:128], hT[:, kc, tt * 128:(tt + 1) * 128], wt[:, kc, :],
                    start=(kc == 0), stop=(kc == KC - 1))
            evac(p_[:, 0:128], tt)

    NCH = NT // 64

    def hgrn_head(l, hd):
        base = N_A // 128
        with k.scope():
            qT = k.sb("hq", [128, NT], F32)
            sg = k.sb("hsg", [128, NT], F32)
            f = k.sb("hf", [128, NT], F32)
            tA = k.sb("hA", [128, NT], F32)
            tB = k.sb("hB", [128, NT], F32)
            tC = k.sb("hC", [128, NT], F32)
            tD = k.sb("hD", [128, NT], F32)
            X1 = k.sb("hX1", [128, NT], F32)
            X2 = k.sb("hX2", [128, NT], F32)
            oT = k.sb("ho", [128, NT], F32)
            vtok = k.sb("hv", [128, NT // 128, 128], F32)
            eb = k.sb("heb", [128, NCH], F32)
            hng_s = k.sb("hng_s", [128, 1], F32)
            Sb = [k.sb("hS%d" % i, [128, 128], F32) for i in range(2)]
            attb = [k.sb("hat%d" % i, [128, 128], F32) for i in range(2)]
            att2 = k.sb("hat2", [128, 128], F32)
            kbtb = [k.sb("hkb%d" % i, [128, 128], F32) for i in range(2)]
            ybf = [k.sb("hy%d" % i, [128, 512], BF16) for i in range(2)]
            tmp5 = k.sb("ht5", [128, 512], F32)
            rs5 = k.sb("hr5", [128, 512], F32)
            k.dma("sp", hng_s[:], hng.a[l])
            proj_tile(l, base + hd, lambda p, t0, tn: k.I("act", "activation", qT[:, t0:t0 + tn], p, AF.Silu))
            proj_tile(l, base + 64 + hd, lambda p, t0, tn: k.I("act", "activation", sg[:, t0:t0 + tn], p, AF.Silu))
            proj_tok(l, base + 48 + hd, lambda p, tt: k.I("dve", "tensor_copy", vtok[:, tt, :], p))
            lbp = lb_s[:, hd, 0:1]
            omp = oml_s[:, hd, 0:1]
            NS = NT // 32

            def v3(t_):
                return t_[:].r("p (c t) -> p c t", t=32)

            def v4(t_):
                return t_[:].r("p (c a t) -> p c a t", a=2, t=32)
            for j in range(2):
                proj_tile(l, base + 16 * (1 + j) + hd,
                          lambda p, t0, tn: k.I("act", "activation", f[:, t0:t0 + tn], p, AF.Sigmoid))
                k.I("dve", "tensor_scalar", f[:], f[:], omp, lbp, ALU.mult, ALU.add)
                k.I("act", "activation", tA[:], f[:], AF.Ln)
                k.I("dve", "tensor_scalar", f[:], f[:], -1.0, 1.0, ALU.mult, ALU.add)
                k.I("dve", "tensor_tensor_scan", tB[:], ones_f[:, 0:1].bc([128, NT]), tA[:], 0.0, ALU.mult, ALU.add)
                G3, g3, lc3 = v3(tB), v3(tA), v3(tC)
                lc4, bc4 = v4(tC), v4(tD)
                if j == 0:
                    k.I("dve", "tensor_copy", lc3[:, 0:1, :], G3[:, 0:1, :])
                    k.I("dve", "tensor_tensor", lc3[:, 1:NS, :], G3[:, 1:NS, :],
                        G3[:, 0:NS - 1, 31:32].bc([128, NS - 1, 32]), ALU.subtract)
                    first, second, last = 0, 1, 31
                else:
                    k.I("dve", "tensor_tensor", g3, g3, G3, ALU.subtract)
                    k.I("dve", "tensor_tensor", lc3, g3, G3[:, :, 31:32].bc([128, NS, 32]), ALU.add)
                    first, second, last = 1, 0, 0
                k.I("dve", "tensor_copy", bc4[:, :, first, :], lc4[:, :, first, :])
                k.I("dve", "tensor_tensor", bc4[:, :, second, :], lc4[:, :, second, :],
                    lc4[:, :, first, last:last + 1].bc([128, NCH, 32]), ALU.add)
                k.I("act", "activation", eb[:], bc4[:, :, second, last], AF.Exp)
                k.I("act", "activation", tA[:], tC[:], AF.Exp)
                k.I("dve", "tensor_tensor", X1[:], qT[:], tA[:], ALU.mult)
                k.I("act", "activation", tA[:], tC[:], AF.Exp, scale=-1.0)
                k.I("dve", "tensor_tensor", X2[:], f[:], tA[:], ALU.mult)
                k.I("dve", "tensor_tensor", v3(tB), lc3[:, :, last:last + 1].bc([128, NS, 32]), lc3, ALU.subtract)
                k.I("act", "activation", tB[:], tB[:], AF.Exp)
                k.I("dve", "tensor_tensor", tC[:], f[:], tB[:], ALU.mult)
                k.I("act", "activation", tA[:], tD[:], AF.Exp)
                k.I("dve", "tensor_tensor", tB[:], qT[:], tA[:], ALU.mult)
                bc3c = tD[:].r("p (c t) -> p c t", t=64)
                k.I("dve", "tensor_tensor", bc3c, bc4[:, :, second, last:last + 1].bc([128, NCH, 64]), bc3c, ALU.subtract)
                k.I("act", "activation", tD[:], tD[:], AF.Exp)
                k.I("dve", "tensor_tensor", f[:], f[:], tD[:], ALU.mult)
                qloc, kloc, kk2, qfull, kbar = X1, X2, tC, tB, f
                order = list(range(NT // 128)) if j == 0 else [1, 0] + list(range(NT // 128 - 1, 1, -1))
                cur = 0
                k.I("dve", "memset", Sb[0][:], 0.0)
                for it, tt in enumerate(order):
                    tok = slice(tt * 128, tt * 128 + 128)
                    pa = next_ps()
                    k.I("pe", "matmul", pa[:, 0:128], kloc[:, tok], qloc[:, tok], start=True, stop=True)
                    pa2 = next_ps()
                    k.I("pe", "matmul", pa2[:, 0:128], kk2[:, tok], qloc[:, tok], start=True, stop=True)
                    attT = attb[it % 2]
                    k.I("dve", "tensor_tensor", attT[:], pa[:, 0:128], cst[:, 6 + j, :], ALU.mult)
                    k.I("dve", "tensor_tensor", att2[:], pa2[:, 0:128], cst[:, 8 + j, :], ALU.mult)
                    k.I("dve", "tensor_tensor", attT[:], attT[:], att2[:], ALU.add)
                    pt = next_ps()
                    k.I("pe", "transpose", pt[:, 0:128], kbar[:, tok], ident)
                    kbt = kbtb[it % 2]
                    k.I("act", "copy", kbt[:], pt[:, 0:128])
                    po = next_ps()
                    k.I("pe", "matmul", po[:, 0:128], vtok[:, tt, :], attT[:], start=True, stop=False)
                    halves = [0, 1] if j == 0 else [1, 0]
                    for hi, hf in enumerate(halves):
                        c = tt * 2 + hf
                        k.I("pe", "matmul", po[:, hf * 64:hf * 64 + 64], Sb[cur][:],
                            qfull[:, tt * 128 + hf * 64:tt * 128 + hf * 64 + 64], start=False, stop=(hi == 1))
                        psu = next_ps()
                        k.I("pe", "matmul", psu[:, 0:128], kbt[hf * 64:hf * 64 + 64, :],
                            vtok[hf * 64:hf * 64 + 64, tt, :], start=True, stop=True)
                        k.I("dve", "scalar_tensor_tensor", Sb[1 - cur][:], Sb[cur][:], eb[:, c:c + 1], psu[:, 0:128],
                            ALU.mult, ALU.add)
                        cur = 1 - cur
                    if j == 0:
                        k.I("act", "copy", oT[:, tok], po[:, 0:128])
                    else:
                        k.I("dve", "tensor_tensor", oT[:, tok], oT[:, tok], po[:, 0:128], ALU.add)
            for ib, (t0, tn) in enumerate(TBLK):
                k.I("act", "activation", tmp5[:, 0:tn], oT[:, t0:t0 + tn], AF.Square)
                pss = next_ps()
                k.I("pe", "matmul", pss[:, 0:tn], ones_f[:], tmp5[:, 0:tn], start=True, stop=True)
                k.I("act", "activation", rs5[:, 0:tn], pss[:, 0:tn], AF.Sqrt, bias=eps_t[:, 1:2], scale=1.0 / 128)
                k.I("dve", "reciprocal", rs5[:, 0:tn], rs5[:, 0:tn])
                k.I("dve", "tensor_tensor", tmp5[:, 0:tn], oT[:, t0:t0 + tn], rs5[:, 0:tn], ALU.mult)
                yb_ = ybf[ib % 2]
                k.I("dve", "scalar_tensor_tensor", yb_[:, 0:tn], tmp5[:, 0:tn], hng_s[:, 0:1], sg[:, t0:t0 + tn],
                    ALU.mult, ALU.mult)
                k.dma("sp", Y.a[1, hd * 128:(hd + 1) * 128, t0:t0 + tn], yb_[:, 0:tn])

    rpar = k.dram("rpar", [NL, 128, 9, KC], F32, "ExternalInput")
    shw = k.dram("shw", [NL, 48, 128, 3], F32, "ExternalInput")
    shwl = k.dram("shwl", [NL, 4, 96, 3], F32, "ExternalInput")
    winl = k.dram("winl", [NL, 4, 128, KC, 96], F32, "ExternalInput")
    lup = k.dram("lup", [NL, 4, 96, D], F32, "ExternalInput")
    bdm = cst[:, 5, :]
    bdm3 = cst[:, 5, :].r("p (h s) -> p h s", h=2)

    def conv_shift(dst, praw, w3):
        k.I("dve", "tensor_scalar", dst, praw, w3[:, 1:2], None, ALU.mult)
        for (a, b_) in ((0, CTX), (CTX, NT)):
            k.I("dve", "scalar_tensor_tensor", dst[:, a + 1:b_], praw[:, a:b_ - 1], w3[:, 0:1], dst[:, a + 1:b_],
                ALU.mult, ALU.add)
            k.I("dve", "scalar_tensor_tensor", dst[:, a:b_ - 1], praw[:, a + 1:b_], w3[:, 2:3], dst[:, a:b_ - 1],
                ALU.mult, ALU.add)

    def rwkv_layer(l, pairs):
        with k.scope():
            rp = k.sb("rp", [128, 9, KC], F32)
            k.dma("sp", rp[:], rpar.a[l])
            lora = [k.sb("lora%d" % i, [96, NT], BF16) for i in range(4)]
            with k.scope():
                praw = k.sb("lpraw", [96, NT], F32)
                ltmp = k.sb("ltmp", [96, NT], F32)
                sw = k.sb("lsw", [96, 4, 3], F32)
                k.dma("sp", sw[:], shwl.a[l].r("j p t -> p j t"))
                for i in range(4):
                    wt = wbuf[st["wb"] % len(wbuf)]
                    st["wb"] += 1
                    k.dma("pool", wt[:, :, 0:96], winl.a[l, i])
                    for (t0, tn) in TBLK:
                        p_ = next_ps()
                        for kc in range(KC):
                            k.I("pe", "matmul", p_[0:96, 0:tn], wt[:, kc, 0:96], hT[:, kc, t0:t0 + tn],
                                start=(kc == 0), stop=(kc == KC - 1))
                        k.I("act", "copy", praw[:, t0:t0 + tn], p_[0:96, 0:tn])
                    conv_shift(ltmp[:], praw[:], sw[:, i, :])
                    if i < 2:
                        k.I("act", "activation", lora[i][:], ltmp[:], AF.Tanh)
                    else:
                        k.I("act", "copy", lora[i][:], ltmp[:])
            for hp in pairs:
                rwkv_pair(l, hp, rp, lora)

    def rwkv_pair(l, hp, rp, lora):
        par = lambda i: rp[:, i, hp:hp + 1]
        with k.scope():
            rT = k.sb("rr", [128, NT], F32)
            kT = k.sb("rk", [128, NT], F32)
            vT = k.sb("rv", [128, NT], F32)
            kkT = k.sb("rkk", [128, NT], F32)
            oT = k.sb("ro", [128, NT], F32)
            bnT = k.sb("rbn", [128, NT], BF16)
            sg = k.sb("rsg", [128, NT], BF16)
            lupt = k.sb("rlup", [96, 4, 128], BF16)
            sw = k.sb("rsw", [128, 3, 3], F32)
            k.dma("pool", lupt[:], lup.a[l, :, :, hp * 128:(hp + 1) * 128].r("j r c -> r j c"))
            for i in range(3):
                k.dma("sp", sw[:, i, :], shw.a[l, i * 16 + hp])

            def lora_act(dst, j, which, func):
                idx = which * 2 + j
                for (t0, tn) in TBLK:
                    p_ = next_ps()
                    k.I("pe", "matmul", p_[:, 0:tn], lupt[:, idx, :], lora[idx][:, t0:t0 + tn], start=True, stop=True)
                    k.I("act", "activation", dst[:, t0:t0 + tn], p_[:, 0:tn], func, bias=par(idx), scale=1.0)
            with k.scope():
                praw = k.sb("rpraw", [128, NT], F32)
                a0T = k.sb("ra0", [128, NT], F32)
                a1T = k.sb("ra1", [128, NT], F32)
                for i, dst in enumerate((rT, kT, vT)):
                    proj_tile(l, i * 16 + hp, lambda p, t0, tn: k.I("act", "copy", praw[:, t0:t0 + tn], p))
                    conv_shift(dst[:], praw[:], sw[:, i, :])
                proj_tile(l, N_SHIFT // 128 + hp, lambda p, t0, tn: k.I("act", "activation", sg[:, t0:t0 + tn], p, AF.Silu))
                lora_act(a0T, 0, 1, AF.Sigmoid)
                lora_act(a1T, 1, 1, AF.Sigmoid)
                k.I("dve", "tensor_scalar", kkT[:], kT[:], par(4), None, ALU.mult)
                k.I("act", "activation", praw[:], kkT[:], AF.Square)
                for (t0, tn) in TBLK:
                    p_ = next_ps()
                    k.I("pe", "matmul", p_[:, 0:tn], bdm, praw[:, t0:t0 + tn], start=True, stop=True)
                    k.I("act", "activation", a1T[:, t0:t0 + tn], p_[:, 0:tn], AF.Sqrt)
                k.I("dve", "tensor_scalar", praw[:], a1T[:], 1e-12, None, ALU.max)
                k.I("dve", "reciprocal", praw[:], praw[:])
                k.I("dve", "tensor_tensor", kkT[:], kkT[:], praw[:], ALU.mult)
                lora_act(a1T, 1, 1, AF.Sigmoid)
                k.I("dve", "tensor_tensor", a0T[:], a0T[:], a1T[:], ALU.add)
                k.I("dve", "tensor_scalar", a0T[:], a0T[:], -2.0, par(5), ALU.add, ALU.mult)
                k.I("dve", "scalar_tensor_tensor", a0T[:], a0T[:], 2.0, kT[:], ALU.add, ALU.mult)
                k.I("dve", "scalar_tensor_tensor", a0T[:], rT[:], par(6), a0T[:], ALU.mult, ALU.mult)
                for (t0, tn) in TBLK:
                    p_ = next_ps()
                    k.I("pe", "matmul", p_[:, 0:tn], bdm, a0T[:, t0:t0 + tn], start=True, stop=True)
                    k.I("dve", "tensor_tensor", bnT[:, t0:t0 + tn], p_[:, 0:tn], vT[:, t0:t0 + tn], ALU.mult)
            for j in range(2):
                with k.scope():
                    rwkv_dir(l, hp, j, par, lora_act, rT, kT, vT, kkT, oT)
            with k.scope():
                xc = k.sb("rxc", [128, 512], F32)
                sq = k.sb("rsq", [128, 512], F32)
                rs = k.sb("rrs", [128, 512], F32)
                ybf = [k.sb("ry%d" % i, [128, 512], BF16) for i in range(2)]
                geps = k.sb("rgeps", [128, 1], F32)
                k.I("dve", "memset", geps[:], 64e-5)
                for ib, (t0, tn) in enumerate(TBLK):
                    pm = next_ps()
                    k.I("pe", "matmul", pm[:, 0:tn], bdm, oT[:, t0:t0 + tn], start=True, stop=True)
                    k.I("dve", "scalar_tensor_tensor", xc[:, 0:tn], pm[:, 0:tn], -1.0 / 64, oT[:, t0:t0 + tn],
                        ALU.mult, ALU.add)
                    k.I("act", "activation", sq[:, 0:tn], xc[:, 0:tn], AF.Square)
                    pv = next_ps()
                    k.I("pe", "matmul", pv[:, 0:tn], bdm, sq[:, 0:tn], start=True, stop=True)
                    k.I("act", "activation", rs[:, 0:tn], pv[:, 0:tn], AF.Sqrt, bias=geps[:], scale=1.0 / 64)
                    k.I("dve", "reciprocal", rs[:, 0:tn], rs[:, 0:tn])
                    k.I("dve", "tensor_tensor", xc[:, 0:tn], xc[:, 0:tn], rs[:, 0:tn], ALU.mult)
                    k.I("dve", "tensor_scalar", xc[:, 0:tn], xc[:, 0:tn], par(7), par(8), ALU.mult, ALU.add)
                    k.I("dve", "tensor_tensor", xc[:, 0:tn], xc[:, 0:tn], bnT[:, t0:t0 + tn], ALU.add)
                    yb_ = ybf[ib % 2]
                    k.I("dve", "tensor_tensor", yb_[:, 0:tn], xc[:, 0:tn], sg[:, t0:t0 + tn], ALU.mult)
                    k.dma("sp", Y.a[0, hp * 128:(hp + 1) * 128, t0:t0 + tn], yb_[:, 0:tn])

    def rwkv_dir(l, hp, j, par, lora_act, rT, kT, vT, kkT, oT):
        aT = k.sb("ra", [128, NT], F32)
        lwT = k.sb("rlw", [128, NT], F32)
        lora_act(aT, j, 1, AF.Sigmoid)
        lora_act(lwT, j, 0, AF.Sigmoid)
        k.I("dve", "tensor_scalar", lwT[:], lwT[:], -0.6065306597126334, None, ALU.mult)
        NB = 1

        def buf(name, shape=(128, 128), n=NB):
            return [k.sb("%s%d" % (name, i), list(shape), F32) for i in range(n)]
        c64 = (128, 64)
        Lb, Lxb, E1b, E2b, E3b, E4b = (buf(n_, c64) for n_ in ("rL", "rLx", "rE1", "rE2", "rE3", "rE4"))
        totb = buf("rtot", (128, 2))
        t1b, kjb, bjb, Atb, Rtb = (buf(n_, c64) for n_ in ("rt1", "rkj", "rbj", "rAt", "rRt"))
        E2d, E3d, E4d, Abd, Khd, Bhd, Kbd, Bbd, Vfd = (buf(n_) for n_ in
                                                       ("rE2d", "rE3d", "rE4d", "rAbd", "rKhd", "rBhd", "rKbd", "rBbd", "rVfd"))
        Nmb, NmTb, Pb, Qb, QTb = (buf(n_) for n_ in ("rNm", "rNmT", "rP", "rQ", "rQT"))
        Mrk, Mrb = buf("rMrk", c64), buf("rMrb", c64)
        Makd = buf("rMak")
        Wst, Ust, Vst = buf("rW", c64), buf("rU", c64), buf("rVs", c64)
        Ubd, Vbd, Kt, Bt = buf("rUbd"), buf("rVbd"), buf("rKt"), buf("rBt")
        Hst = buf("rHs", c64, 2)
        Hbd = buf("rHbd", (128, 128), 2)
        for hb in Hbd:
            k.I("dve", "memset", hb[:], 0.0)
        for ub in Ubd:
            k.I("dve", "memset", ub[:], 0.0)
        k.I("dve", "memset", Hst[0][:], 0.0)
        hcur = 0
        if j == 0:
            chunks = list(range(NCH))
            ms = 3
            m_incl = cst[:, 10, 0:64]
        else:
            chunks = list(range(CTX // 64 - 1, -1, -1)) + list(range(NCH - 1, CTX // 64 - 1, -1))
            ms = 4
            m_incl = cst[:, 10, 64:128]
        m_str_bd = cst[:, ms, :]
        m_strT_bd = cst[:, 7 - ms, :]
        bd3 = lambda t_: t_[:].r("p (h s) -> p h s", h=2)
        bcx = lambda v_: v_.us(1).bc([128, 2, 64])
        for it, c in enumerate(chunks):
            b_ = it % NB
            tk = slice(c * 64, c * 64 + 64)
            L, Lx, E1, E2, E3, E4, tot = Lb[b_], Lxb[b_], E1b[b_], E2b[b_], E3b[b_], E4b[b_], totb[b_]
            k.I("dve", "tensor_tensor_scan", L[:], ones_f[:, 0:64], lwT[:, tk], 0.0, ALU.mult, ALU.add)
            k.I("dve", "tensor_copy", tot[:, 0:1], L[:, 63:64])
            if j == 1:
                k.I("dve", "tensor_tensor", L[:], lwT[:, tk], L[:], ALU.subtract)
                k.I("dve", "tensor_scalar", L[:], L[:], tot[:, 0:1], None, ALU.add)
            k.I("dve", "tensor_tensor", Lx[:], L[:], lwT[:, tk], ALU.subtract)
            k.I("act", "activation", E1[:], L[:], AF.Exp)
            k.I("act", "activation", E2[:], Lx[:], AF.Exp)
            k.I("act", "activation", E3[:], L[:], AF.Exp, scale=-1.0)
            k.I("act", "activation", E4[:], L[:], AF.Exp, bias=tot[:, 0:1], scale=-1.0)
            k.I("act", "activation", tot[:, 1:2], tot[:, 0:1], AF.Exp)
            for src, dst in ((E2, E2d[b_]), (E3, E3d[b_]), (E4, E4d[b_])):
                k.I("pool", "tensor_tensor", bd3(dst), bcx(src[:]), bdm3, ALU.mult)
            t1, kj, bj, At, Rt = t1b[b_], kjb[b_], bjb[b_], Atb[b_], Rtb[b_]
            k.I("dve", "tensor_scalar", t1[:], aT[:, tk], -1.0, par(5), ALU.add, ALU.mult)
            k.I("dve", "scalar_tensor_tensor", kj[:], t1[:], 1.0, kT[:, tk], ALU.add, ALU.mult)
            k.I("dve", "tensor_tensor", bj[:], kkT[:, tk], aT[:, tk], ALU.mult)
            k.I("dve", "tensor_tensor", At[:], kkT[:, tk], E2[:], ALU.mult)
            k.I("dve", "tensor_tensor", Rt[:], rT[:, tk], E1[:], ALU.mult)
            k.I("pool", "tensor_tensor", bd3(Abd[b_]), bcx(kkT[:, tk]), bd3(E2d[b_]), ALU.mult)
            k.I("dve", "tensor_tensor", bd3(Khd[b_]), bcx(kj[:]), bd3(E3d[b_]), ALU.mult)
            k.I("pool", "tensor_tensor", bd3(Bhd[b_]), bcx(bj[:]), bd3(E3d[b_]), ALU.mult)
            k.I("dve", "tensor_tensor", bd3(Kbd[b_]), bcx(kj[:]), bd3(E4d[b_]), ALU.mult)
            k.I("pool", "tensor_tensor", bd3(Bbd[b_]), bcx(bj[:]), bd3(E4d[b_]), ALU.mult)
            k.I("pool", "tensor_tensor", bd3(Vfd[b_]), bcx(vT[:, tk]), bdm3, ALU.mult)
            for src, dst in ((Kbd[b_], Kt[b_]), (Bbd[b_], Bt[b_]), (Vfd[b_], Vbd[b_])):
                pt = next_ps()
                k.I("pe", "transpose", pt[:, 0:128], src[:], ident)
                k.I("act", "copy", dst[:], pt[:, 0:128])
            for h2 in range(2):
                k.I("dve", "tensor_copy", Vst[b_][h2 * 64:(h2 + 1) * 64, :], Vbd[b_][h2 * 64:(h2 + 1) * 64, h2 * 64:(h2 + 1) * 64])
            pn = next_ps()
            k.I("pe", "matmul", pn[:, 0:128], Bhd[b_][:], Abd[b_][:], start=True, stop=True)
            pnt = next_ps()
            k.I("pe", "matmul", pnt[:, 0:128], Abd[b_][:], Bhd[b_][:], start=True, stop=True)
            Nm, NmT, P_, Q, QT = Nmb[b_], NmTb[b_], Pb[b_], Qb[b_], QTb[b_]
            k.I("dve", "tensor_tensor", Nm[:], pn[:, 0:128], m_str_bd, ALU.mult)
            k.I("dve", "tensor_tensor", NmT[:], pnt[:, 0:128], m_strT_bd, ALU.mult)
            k.I("dve", "tensor_tensor", P_[:], ident, Nm[:], ALU.subtract)
            X, XT = Nm, NmT
            for lev in range(5):
                pq = next_ps()
                k.I("pe", "matmul", pq[:, 0:128], XT[:], X[:], start=True, stop=True)
                pqt = next_ps()
                k.I("pe", "matmul", pqt[:, 0:128], X[:], XT[:], start=True, stop=True)
                if lev % 2 == 0:
                    nQ, nQT = Q, QT
                else:
                    nQ, nQT = Nm, NmT
                k.I("act", "copy", nQ[:], pq[:, 0:128])
                k.I("act", "copy", nQT[:], pqt[:, 0:128])
                X, XT = nQ, nQT
                pp = next_ps()
                k.I("pe", "matmul", pp[:, 0:128], XT[:], P_[:], start=True, stop=True)
                k.I("dve", "tensor_tensor", P_[:], P_[:], pp[:, 0:128], ALU.add)
            TT = P_
            p1 = next_ps()
            k.I("pe", "matmul", p1[:, 0:64], Khd[b_][:], Rt[:], start=True, stop=True)
            k.I("dve", "tensor_tensor", Mrk[b_][:], p1[:, 0:64], m_incl, ALU.mult)
            p2 = next_ps()
            k.I("pe", "matmul", p2[:, 0:64], Bhd[b_][:], Rt[:], start=True, stop=True)
            k.I("dve", "tensor_tensor", Mrb[b_][:], p2[:, 0:64], m_incl, ALU.mult)
            p3 = next_ps()
            k.I("pe", "matmul", p3[:, 0:128], Khd[b_][:], Abd[b_][:], start=True, stop=True)
            k.I("dve", "tensor_tensor", Makd[b_][:], p3[:, 0:128], m_str_bd, ALU.mult)
            pw = next_ps()
            k.I("pe", "matmul", pw[:, 0:64], Abd[b_][:], Hst[hcur][:], start=True, stop=False)
            k.I("pe", "matmul", pw[:, 0:64], Makd[b_][:], Vst[b_][:], start=False, stop=True)
            k.I("act", "copy", Wst[b_][:], pw[:, 0:64])
            pu = next_ps()
            k.I("pe", "matmul", pu[:, 0:64], TT[:], Wst[b_][:], start=True, stop=True)
            k.I("dve", "tensor_scalar", Ust[b_][:], pu[:, 0:64], -1.0, None, ALU.mult)
            for h2 in range(2):
                k.I("dve", "tensor_scalar", Ubd[b_][h2 * 64:(h2 + 1) * 64, h2 * 64:(h2 + 1) * 64],
                    pu[h2 * 64:(h2 + 1) * 64, 0:64], -1.0, None, ALU.mult)
            po = next_ps()
            k.I("pe", "matmul", po[:, 0:64], Hbd[hcur][:], Rt[:], start=True, stop=False)
            k.I("pe", "matmul", po[:, 0:64], Vbd[b_][:], Mrk[b_][:], start=False, stop=False)
            k.I("pe", "matmul", po[:, 0:64], Ubd[b_][:], Mrb[b_][:], start=False, stop=True)
            if j == 0:
                k.I("act", "copy", oT[:, tk], po[:, 0:64])
            else:
                k.I("dve", "tensor_tensor", oT[:, tk], oT[:, tk], po[:, 0:64], ALU.add)
            ph = next_ps()
            k.I("pe", "matmul", ph[:, 0:64], Kt[b_][:], Vst[b_][:], start=True, stop=False)
            k.I("pe", "matmul", ph[:, 0:64], Bt[b_][:], Ust[b_][:], start=False, stop=True)
            hn = 1 - hcur
            k.I("dve", "scalar_tensor_tensor", Hst[hn][:], Hst[hcur][:], tot[:, 1:2], ph[:, 0:64], ALU.mult, ALU.add)
            for h2 in range(2):
                k.I("act", "copy", Hbd[hn][h2 * 64:(h2 + 1) * 64, h2 * 64:(h2 + 1) * 64], Hst[hn][h2 * 64:(h2 + 1) * 64, :])
            hcur = hn

    nab = k.dram("nab", [NL, 16, 64, 16, 64], F32, "ExternalInput")
    Gs = k.dram("Gs", [3, D, NT], BF16, "Internal")
    wbr = k.dram("wbr", [NL, 3, KC, 128, KC, 128], F32, "ExternalInput")
    wo = k.dram("wo", [NL, KC, 128, KC, 128], F32, "ExternalInput")
    fg = k.dram("fg", [128, KC], F32, "ExternalInput")
    ones_b = k.sb("ones_b", [128, 128], BF16)
    k.I("dve", "memset", ones_b[:], 1.0)
    NA_SCALE = 128.0 ** -0.5
    NEG = -30000.0

    def na_types():
        def kps(R):
            lo = min(max(2 * R - 4, 0), 24) // 2
            hi = (min(max(2 * R + 1 - 4, 0), 24) + 7) // 2
            return list(range(lo, hi + 1))
        return kps

    def na_head(l, hd):
        base = (N_A + N_B) // 128
        kps = na_types()
        with k.scope():
            qT = k.sb("nq", [128, NT], BF16)
            kT = k.sb("nk", [128, NT], BF16)
            vtok = k.sb("nv", [128, NT // 128, 128], BF16)
            sg = k.sb("nsg", [128, NT], BF16)
            yT = k.sb("ny", [128, NT], BF16)
            blk2 = k.sb("nblk", [128, 16, 64], F32)
            btile = k.sb("nbt", [128, 21, 128], F32)
            tmpA = k.sb("ntA", [128, 512], F32)
            tmpB = k.sb("ntB", [128, 128], F32)
            PTA = [k.sb("nPA%d" % i, [128, 512], BF16) for i in range(2)]
            PTB = [k.sb("nPB%d" % i, [128, 384], BF16) for i in range(2)]
            rinv = k.sb("nri", [128, 256], F32)
            otmp = k.sb("not", [128, 256], F32)
            for half in range(2):
                k.dma("sp", blk2[half * 64:(half + 1) * 64, :, :], nab.a[l, hd])
            proj_tile(l, base + hd, lambda p, t0, tn: k.I("act", "mul", qT[:, t0:t0 + tn], p, NA_SCALE))
            proj_tile(l, base + 16 + hd, lambda p, t0, tn: k.I("act", "copy", kT[:, t0:t0 + tn], p))
            proj_tok(l, base + 32 + hd, lambda p, tt: k.I("dve", "tensor_copy", vtok[:, tt, :], p))
            proj_tile(l, base + 48 + hd, lambda p, t0, tn: k.I("act", "activation", sg[:, t0:t0 + tn], p, AF.Silu))
            toff = {}
            ti = 0
            for R in (0, 1, 7, 14, 15):
                toff[R] = ti
                for Kp in kps(R):
                    for ki in range(2):
                        for qi in range(2):
                            krow, qrow = 2 * Kp + ki, 2 * R + qi
                            rsq = min(max(qrow - 4, 0), 24)
                            idx = (krow - qrow + 7) if (rsq <= krow < rsq + 8) else 15
                            k.I("pool", "tensor_copy", btile[ki * 64:(ki + 1) * 64, ti, qi * 64:(qi + 1) * 64],
                                blk2[ki * 64:(ki + 1) * 64, idx, :])
                    ti += 1

            def finish(po, pr, t0, n):
                k.I("dve", "reciprocal", rinv[:, 0:n], pr[:, 0:n])
                k.I("dve", "tensor_tensor", otmp[:, 0:n], po[:, 0:n], rinv[:, 0:n], ALU.mult)
                k.I("dve", "tensor_tensor", yT[:, t0:t0 + n], otmp[:, 0:n], sg[:, t0:t0 + n], ALU.mult)
            psc = next_ps()
            for kc2 in range(2):
                k.I("pe", "matmul", psc[:, kc2 * 256:(kc2 + 1) * 256], kT[:, kc2 * 128:(kc2 + 1) * 128], qT[:, 0:256],
                    start=True, stop=True)
            k.I("act", "activation", PTA[0][:], psc[:, 0:512], AF.Exp)
            po, pr = next_ps(), next_ps()
            for kc2 in range(2):
                k.I("pe", "matmul", po[:, 0:256], vtok[:, kc2, :], PTA[0][:, kc2 * 256:(kc2 + 1) * 256],
                    start=(kc2 == 0), stop=(kc2 == 1))
            for kc2 in range(2):
                k.I("pe", "matmul", pr[:, 0:256], ones_b[:], PTA[0][:, kc2 * 256:(kc2 + 1) * 256],
                    start=(kc2 == 0), stop=(kc2 == 1))
            finish(po, pr, 0, 256)
            for R in range(16):
                Kl = kps(R)
                typ = R if R in (0, 1, 14, 15) else 7
                bo = toff[typ]
                qs = slice(256 + R * 128, 256 + R * 128 + 128)
                pA, pB = next_ps(), next_ps()
                nl = len(Kl)
                for i, Kp in enumerate(Kl):
                    dst = pA[:, i * 128:(i + 1) * 128] if i < 4 else pB[:, 0:128]
                    k.I("pe", "matmul", dst, kT[:, 256 + Kp * 128:256 + Kp * 128 + 128], qT[:, qs], start=True, stop=True)
                cb = 128 if nl == 5 else 0
                for kc2 in range(2):
                    k.I("pe", "matmul", pB[:, cb + kc2 * 128:cb + (kc2 + 1) * 128], kT[:, kc2 * 128:(kc2 + 1) * 128],
                        qT[:, qs], start=True, stop=True)
                PA, PB = PTA[R % 2], PTB[R % 2]
                k.I("dve", "tensor_tensor", tmpA[:], pA[:, 0:512], btile[:, bo:bo + 4, :].r("p a b -> p (a b)"), ALU.add)
                k.I("act", "activation", PA[:], tmpA[:], AF.Exp)
                if nl == 5:
                    k.I("dve", "tensor_tensor", tmpB[:], pB[:, 0:128], btile[:, bo + 4, :], ALU.add)
                    k.I("act", "activation", PB[:, 0:128], tmpB[:], AF.Exp)
                k.I("act", "activation", PB[:, cb:cb + 256], pB[:, cb:cb + 256], AF.Exp)
                srcs = []
                for i, Kp in enumerate(Kl):
                    srcs.append((2 + Kp, PA[:, i * 128:(i + 1) * 128] if i < 4 else PB[:, 0:128]))
                for kc2 in range(2):
                    srcs.append((kc2, PB[:, cb + kc2 * 128:cb + (kc2 + 1) * 128]))
                po, pr = next_ps(), next_ps()
                for i, (tt, pv) in enumerate(srcs):
                    k.I("pe", "matmul", po[:, 0:128], vtok[:, tt, :], pv, start=(i == 0), stop=(i == len(srcs) - 1))
                for i, (tt, pv) in enumerate(srcs):
                    k.I("pe", "matmul", pr[:, 0:128], ones_b[:], pv, start=(i == 0), stop=(i == len(srcs) - 1))
                finish(po, pr, 256 + R * 128, 128)
            k.dma("sp", Y.a[2, hd * 128:(hd + 1) * 128, :], yT[:])

    def gates(l):
        gt0 = (N_A + N_B + N_C) // 128
        with k.scope():
            gb = [k.sb("gb%d" % i, [128, NT], BF16) for i in range(2)]
            for i in range(48):
                g_ = gb[i % 2]
                proj_tile(l, gt0 + i, lambda p, t0, tn: k.I("act", "activation", g_[:, t0:t0 + tn], p, AF.Sigmoid))
                k.dma("sp", Gs.a[i // 16, (i % 16) * 128:(i % 16 + 1) * 128, :], g_[:])

    def merge(l, last):
        with k.scope():
            yblk = [k.sb("my%d" % i, [128, KC, 512], BF16) for i in range(3)]
            mT = k.sb("mm", [128, KC, 512], BF16)
            g3 = [k.sb("mg%d" % i, [128, 3, 512], BF16) for i in range(2)]
            wb_ = [k.sb("mw%d" % i, [128, KC, 128], BF16) for i in range(4)]
            xo = [k.sb("mx%d" % i, [128, 512], F32) for i in range(2)]
            xn = [k.sb("mxn%d" % i, [128, 512], F32) for i in range(2)]
            acc = k.sb("macc", [128, 512], F32)
            acc2 = k.sb("macc2", [128, 512], F32)
            wi = 0
            for (t0, tn) in TBLK:
                if last and t0 == 0:
                    continue
                col = 1 if t0 == 0 else 0
                for br in range(3):
                    k.dma("sp", yblk[br][:, :, 0:tn], Y.a[br].r("(kc p) n -> p kc n", p=128)[:, :, t0:t0 + tn])
                for oc in range(KC):
                    g_ = g3[oc % 2]
                    k.dma("sp", g_[:, :, 0:tn], Gs.a[:, oc * 128:(oc + 1) * 128, t0:t0 + tn].r("b p n -> p b n"))
                    pss = []
                    for br in range(3):
                        w_ = wb_[wi % 4]
                        wi += 1
                        k.dma("pool", w_[:], wbr.a[l, br, oc])
                        p_ = next_ps()
                        for kc in range(KC):
                            k.I("pe", "matmul", p_[:, 0:tn], w_[:, kc, :], yblk[br][:, kc, 0:tn],
                                start=(kc == 0), stop=(kc == KC - 1))
                        pss.append(p_)
                    k.I("dve", "tensor_tensor", acc[:, 0:tn], pss[0][:, 0:tn], g_[:, 0, 0:tn], ALU.mult)
                    k.I("dve", "tensor_tensor", acc2[:, 0:tn], pss[1][:, 0:tn], g_[:, 1, 0:tn], ALU.mult)
                    k.I("dve", "tensor_tensor", acc[:, 0:tn], acc[:, 0:tn], acc2[:, 0:tn], ALU.add)
                    k.I("dve", "tensor_tensor", acc2[:, 0:tn], pss[2][:, 0:tn], g_[:, 2, 0:tn], ALU.mult)
                    k.I("dve", "tensor_tensor", mT[:, oc, 0:tn], acc[:, 0:tn], acc2[:, 0:tn], ALU.add)
                for oc in range(KC):
                    w_ = wb_[wi % 4]
                    wi += 1
                    k.dma("pool", w_[:], wo.a[l, oc])
                    x_, xn_ = xo[oc % 2], xn[oc % 2]
                    k.dma("sp", x_[:, 0:tn], XT.a[oc * 128:(oc + 1) * 128, t0:t0 + tn])
                    p_ = next_ps()
                    for kc in range(KC):
                        k.I("pe", "matmul", p_[:, 0:tn], w_[:, kc, :], mT[:, kc, 0:tn], start=(kc == 0), stop=(kc == KC - 1))
                    k.I("dve", "scalar_tensor_tensor", xn_[:, 0:tn], p_[:, 0:tn], mods[:, 32 + oc, col:col + 1], x_[:, 0:tn],
                        ALU.mult, ALU.add)
                    k.dma("sp", XT.a[oc * 128:(oc + 1) * 128, t0:t0 + tn], xn_[:, 0:tn])

    def final_norm():
        with k.scope():
            fg_s = k.sb("fg_s", [128, KC], F32)
            k.dma("sp", fg_s[:], fg.a)
            x_ = k.sb("fx", [128, KC, 512], F32)
            sqb = k.sb("fsq", [128, KC, 512], F32)
            rstd = k.sb("frs", [128, 512], F32)
            for (t0, tn) in TBLK[1:]:
                k.dma("sp", x_[:], xt_view(XT)[:, :, t0:t0 + tn])
                k.I("act", "activation", sqb[:], x_[:], AF.Square)
                pss = next_ps()
                for kc in range(KC):
                    k.I("pe", "matmul", pss[:, 0:tn], ones_f[:], sqb[:, kc, :], start=(kc == 0), stop=(kc == KC - 1))
                k.I("act", "activation", rstd[:], pss[:, 0:tn], AF.Sqrt, bias=eps_t[:, 0:1], scale=1.0 / D)
                k.I("dve", "reciprocal", rstd[:], rstd[:])
                for kc in range(KC):
                    k.I("dve", "scalar_tensor_tensor", sqb[:, kc, :], x_[:, kc, :], fg_s[:, kc:kc + 1], rstd[:],
                        ALU.mult, ALU.mult)
                k.dma("sp", out_t.a.r("(kc p) n -> p kc n", p=128)[:, :, t0 - CTX:t0 - CTX + tn], sqb[:])

    tp = dbg.get("_test")
    if tp is not None:
        layer_pre(0, True)
        for hd in tp.get("hgrn", []):
            hgrn_head(0, hd)
        if tp.get("rwkv"):
            rwkv_layer(0, tp["rwkv"])
        for hd in tp.get("na", []):
            na_head(0, hd)
        with k.scope():
            zz = k.sb("zz", [128, SEQ], F32)
            k.I("dve", "memset", zz[:], 0.0)
            for kc in range(KC):
                k.dma("sp", out_t.a[kc * 128:(kc + 1) * 128, :], zz[:])
        return k.finish(), k
    l = 0
    layer_pre(l, True)
    gates(l)
    rwkv_layer(l, list(range(16)))
    for hd in range(16):
        hgrn_head(l, hd)
    for hd in range(16):
        na_head(l, hd)
    k.barrier()
    merge(l, False)
    k.barrier()
    final_norm()
    return k.finish(), k


def na_table(rpb):
    c = np.arange(64)
    cs = np.clip(c - 8, 0, 48)
    cp = np.arange(64)
    valid = (cp[:, None] >= cs[None, :]) & (cp[:, None] < cs[None, :] + 16)
    dc = np.clip(cp[:, None] - c[None, :] + 15, 0, 30)
    H = rpb.shape[0]
    out = np.full((H, 64, 16, 64), -30000.0, np.float32)
    for dr in range(15):
        g = rpb[:, dr, :][:, dc]
        out[:, :, dr, :] = np.where(valid[None], g, np.float32(-30000.0))
    return out


LAYERED = ("wada", "bada", "ng", "win", "rpar", "shw", "shwl", "winl", "lup", "hng", "nab", "wbr", "wo")


def layer_inputs(shared, l):
    d = {}
    for n, v in shared.items():
        d[n] = v[l:l + 1] if n in LAYERED else v
    lm = np.zeros((128, KC, DEPTH), np.float32)
    lm[:, :, 1:l + 1] = 1.0
    d["lmask"] = lm
    return d


def prep_inputs(inp, NL=DEPTH):
    f = np.float32
    x, c, ctx, c_ctx = (np.asarray(inp[n], f) for n in ("x", "c", "ctx", "c_ctx"))
    shared = {}
    w_ada = np.asarray(inp["w_ada"], f)[:NL]
    shared["wada"] = np.ascontiguousarray(w_ada.reshape(NL, KC, 128, 12, 512).transpose(0, 3, 2, 1, 4))
    shared["bada"] = np.ascontiguousarray(np.asarray(inp["b_ada"], f)[:NL].reshape(NL, 48, 128).transpose(0, 2, 1))
    shared["ng"] = np.stack([pc(v) for v in np.asarray(inp["norm_g"], f)[:NL]])
    w_in = np.asarray(inp["w_in"], f)[:NL]
    shared["win"] = np.ascontiguousarray(w_in.reshape(NL, KC, 128, N_IN // 128, 128).transpose(0, 3, 2, 1, 4))
    shared["consts"] = make_consts()
    shared["lbl"] = np.ascontiguousarray(np.asarray(inp["hgrn_lb_logits"], f).reshape(DEPTH, KC, 128).transpose(2, 1, 0))
    g_ = lambda n: np.asarray(inp[n], f)[:NL]
    rp = []
    for l in range(NL):
        vs = [g_("rwkv_w0")[l, 0], g_("rwkv_w0")[l, 1], g_("rwkv_a0")[l, 0], g_("rwkv_a0")[l, 1], g_("rwkv_k_k")[l],
              g_("rwkv_k_a")[l], g_("rwkv_r_k")[l].reshape(-1), g_("rwkv_ln_g")[l], g_("rwkv_ln_b")[l]]
        rp.append(np.stack([pc(v) for v in vs], axis=1))
    shared["rpar"] = np.ascontiguousarray(np.stack(rp))
    sh = g_("rwkv_shift")
    shared["shw"] = np.ascontiguousarray(sh[:, :, :6144].reshape(NL, 3, 48, 128).transpose(0, 2, 3, 1))
    shared["shwl"] = np.ascontiguousarray(sh[:, :, 6144:].reshape(NL, 3, 4, 96).transpose(0, 2, 3, 1))
    shared["winl"] = np.ascontiguousarray(w_in[:, :, 6144:6528].reshape(NL, KC, 128, 4, 96).transpose(0, 3, 2, 1, 4))
    shared["lup"] = np.ascontiguousarray(np.concatenate([g_("rwkv_w_up"), g_("rwkv_a_up")], axis=1))
    shared["hng"] = np.ascontiguousarray(np.asarray(inp["hgrn_norm_g"], f)[:NL].reshape(NL, 128, 1))
    shared["nab"] = np.stack([na_table(r_) for r_ in g_("na_rpb")])
    shared["wbr"] = np.ascontiguousarray(g_("w_branch").reshape(NL, 3, KC, 128, KC, 128).transpose(0, 1, 4, 3, 2, 5))
    shared["wo"] = np.ascontiguousarray(g_("w_out").reshape(NL, KC, 128, KC, 128).transpose(0, 3, 2, 1, 4))
    shared["fg"] = pc(np.asarray(inp["final_g"], f))
    xts = [np.ascontiguousarray(np.concatenate([ctx[b], x[b]], axis=0).T) for b in range(B)]
    cts = [np.ascontiguousarray(np.stack([pc(c[b]), pc(c_ctx)], axis=-1)) for b in range(B)]
    return shared, xts, cts


def kernel(**inputs):
    shared, xts, cts = prep_inputs(inputs)
    nc, _ = build()
    res = None
    for l in range(DEPTH):
        li = layer_inputs(shared, l)
        in_maps = []
        for i in range(B):
            d = dict(li)
            d["xt"] = xts[i]
            d["ct"] = cts[i]
            in_maps.append(d)
        res = run_bass_kernel_spmd(nc, in_maps, core_ids=list(range(B)))
        xts = [np.asarray(res.results[b]["xo"]) for b in range(B)]
    out = np.stack([np.ascontiguousarray(np.asarray(res.results[b]["out"]).T) for b in range(B)])
    return out.astype(np.float32)
```

```python
import contextlib
import numpy as np
import concourse.bass as bass
import concourse.mybir as mybir
from concourse.bass_utils import run_bass_kernel_spmd

F32 = mybir.dt.float32
BF16 = mybir.dt.bfloat16
ALU = mybir.AluOpType
AF = mybir.ActivationFunctionType
AX = mybir.AxisListType

D = 2048
KC = 16
B = 4
SEQ = 2048
CTX = 256
NT = CTX + SEQ
DEPTH = 4
N_SHIFT = 6528
N_A = N_SHIFT + D
N_B = 5 * D
N_C = 4 * D
N_IN = N_A + N_B + N_C + 3 * D
TBLK = [(0, 256), (256, 512), (768, 512), (1280, 512), (1792, 512)]

DEBUG = {}
STOP_AFTER = None


class Tl:
    def __init__(self, handle, name):
        self.h = handle
        self.name = name
        self.w = None
        self.r = {}
        self.is_dram = False

    def __getitem__(self, k):
        return V(self, self.h[k])

    @property
    def a(self):
        return V(self, self.h if self.is_dram else self.h[:])


class V:
    def __init__(self, tile, ap):
        self.tile = tile
        self.ap = ap

    def __getitem__(self, k):
        return V(self.tile, self.ap[k])

    def r(self, s, **kw):
        return V(self.tile, self.ap.rearrange(s, **kw))

    def bc(self, shape):
        return V(self.tile, self.ap.broadcast_to(shape))

    def us(self, ax):
        return V(self.tile, self.ap.unsqueeze(ax))

    def bitcast(self, dt):
        return V(self.tile, self.ap.bitcast(dt))

    @property
    def shape(self):
        return self.ap.shape


ENG = ["pe", "act", "dve", "pool", "sp"]
NSLOT = {"sp": 12, "pool": 12, "act": 6}
ARENA = 212800


class KB:
    def __init__(self):
        self.nc = bass.Bass("TRN2", target_bir_lowering=False)
        self.es = contextlib.ExitStack()
        self.streams = {e: [] for e in ENG}
        self.cnt = {e: 0 for e in ENG}
        self.waited = {e: {} for e in ENG}
        self.sems = {}
        for e in ENG:
            self.sems[e] = self.es.enter_context(self.nc.semaphore("s_" + e))
        self.slot_use = {}
        self.slot_next = {q: 0 for q in NSLOT}
        for q, n in NSLOT.items():
            for i in range(n):
                k = "d_%s_%d" % (q, i)
                self.sems[k] = self.es.enter_context(self.nc.semaphore(k))
                self.slot_use[k] = 0
        self.n_instr = 0
        self.arena = None
        self.sp_ = 0
        self.sp_max = 0

    def sb(self, name, shape, dt):
        if self.arena is None:
            self.arena = self.es.enter_context(self.nc.sbuf_tensor("arena", [128, ARENA], mybir.dt.uint8))
            self.sp_ = 0
        esz = 2 if dt == BF16 else 4
        n = 1
        for d_ in shape[1:]:
            n *= d_
        nb = (n * esz + 63) // 64 * 64
        assert self.sp_ + nb <= ARENA, "SBUF arena overflow at %s: %d + %d" % (name, self.sp_, nb)
        ap = self.arena[0:shape[0], self.sp_:self.sp_ + n * esz].bitcast(dt)
        self.sp_ += nb
        self.sp_max = max(self.sp_max, self.sp_)
        if len(shape) == 3:
            ap = ap.rearrange("p (a b) -> p a b", b=shape[2])
        elif len(shape) == 4:
            ap = ap.rearrange("p (a b c) -> p a b c", b=shape[2], c=shape[3])
        t = Tl(ap, name)
        t.is_dram = True
        return t

    @contextlib.contextmanager
    def scope(self):
        mark = self.sp_
        yield
        self.barrier()
        self.sp_ = mark

    def ps(self, name, shape, dt):
        h = self.es.enter_context(self.nc.psum_tensor(name, list(shape), dt))
        return Tl(h, name)

    def dram(self, name, shape, dt, kind):
        h = self.nc.dram_tensor(name, list(shape), dt, kind=kind)
        t = Tl(h.ap(), name)
        t.is_dram = True
        return t

    def _deps(self, reads, writes):
        deps = {}

        def add(k, v):
            if deps.get(k, 0) < v:
                deps[k] = v
        for t in reads:
            if t.w:
                add(*t.w)
        for t in writes:
            if t.w:
                add(*t.w)
            for k, v in t.r.items():
                add(k, v)
        return deps

    def _waits(self, eng, deps):
        waits = []
        wd = self.waited[eng]
        for k, v in deps.items():
            if k == eng:
                if eng != "pe" and v == self.cnt[eng]:
                    waits.append((k, v))
                continue
            if wd.get(k, 0) < v:
                wd[k] = v
                waits.append((k, v))
        return waits

    def _commit(self, key, val, reads, writes):
        for t in reads:
            if t.r.get(key, 0) < val:
                t.r[key] = val
        for t in writes:
            t.w = (key, val)
            t.r = {}

    def emit(self, eng, fn, reads, writes):
        deps = self._deps(reads, writes)
        waits = self._waits(eng, deps)
        self.cnt[eng] += 1
        self.streams[eng].append((waits, fn, (eng, 1)))
        self._commit(eng, self.cnt[eng], reads, writes)
        self.n_instr += 1

    def I(self, eng, method, *args, **kw):
        reads, writes = [], []

        def conv(x, is_out):
            if isinstance(x, V):
                (writes if is_out else reads).append(x.tile)
                return x.ap
            return x
        a2 = [conv(a, i == 0) for i, a in enumerate(args)]
        k2 = {k: conv(v, k in ("out", "accum_out")) for k, v in kw.items()}
        self.emit(eng, lambda e: getattr(e, method)(*a2, **k2), reads, writes)

    def dma(self, q, out, in_, **kw):
        reads = [in_.tile]
        writes = [out.tile]
        n = NSLOT[q]
        slot = "d_%s_%d" % (q, self.slot_next[q] % n)
        self.slot_next[q] += 1
        deps = self._deps(reads, writes)
        prev = self.slot_use[slot]
        if prev:
            deps[slot] = max(deps.get(slot, 0), prev)
        waits = self._waits(q, deps)
        self.slot_use[slot] = prev + 16
        oa, ia = out.ap, in_.ap
        self.streams[q].append((waits, lambda e: e.dma_start(out=oa, in_=ia, **kw), (slot, 16)))
        self._commit(slot, prev + 16, reads, writes)
        self.n_instr += 1

    def barrier(self):
        tgt = {e: self.cnt[e] for e in ENG}
        for k, v in self.slot_use.items():
            tgt[k] = v
        for e in ENG:
            waits = []
            for k, v in tgt.items():
                if k == e or v == 0:
                    continue
                if self.waited[e].get(k, 0) < v:
                    self.waited[e][k] = v
                    waits.append((k, v))
            if waits:
                self.streams[e].append((waits, None, None))

    def finish(self):
        self.barrier()
        nc = self.nc
        sems = self.sems
        streams = self.streams

        def replay(name):
            def f(e):
                for waits, fn, inc in streams[name]:
                    for k, v in waits:
                        e.wait_ge(sems[k], v)
                    if fn is not None:
                        ins = fn(e)
                        ins.then_inc(sems[inc[0]], inc[1])
            return f
        with nc.Block() as block:
            block.tensor(replay("pe"))
            block.scalar(replay("act"))
            block.vector(replay("dve"))
            block.gpsimd(replay("pool"))
            block.sync(replay("sp"))
        self.es.close()
        return nc


def pc(v):
    return np.ascontiguousarray(np.asarray(v, np.float32).reshape(-1, 128).T)


NCONST = 12


def make_consts():
    c = np.zeros((NCONST, 128, 128), np.float32)
    i = np.arange(128)
    s, t = np.meshgrid(i, i, indexing="ij")
    same = (s // 64) == (t // 64)
    c[0] = np.eye(128)
    c[1] = (same & (s <= t))
    c[2] = (same & (s >= t))
    c[3] = (same & (s < t))
    c[4] = (same & (s > t))
    c[5] = same
    ssub = (s // 32) == (t // 32)
    c[6] = ssub & (s <= t)
    c[7] = ssub & (s >= t)
    c[8] = same & ((s // 32) % 2 == 0) & ((t // 32) % 2 == 1)
    c[9] = same & ((s // 32) % 2 == 1) & ((t // 32) % 2 == 0)
    c[10][:, :64] = (s[:, :64] % 64) <= t[:, :64]
    c[10][:, 64:] = (s[:, :64] % 64) >= t[:, :64]
    return c


def build(dbg=None, NL=1):
    dbg = dbg or {}
    k = KB()
    nc = k.nc
    xt_in = k.dram("xt", [D, NT], F32, "ExternalInput")
    ct_in = k.dram("ct", [128, KC, 2], F32, "ExternalInput")
    wada = k.dram("wada", [NL, 12, 128, KC, 512], F32, "ExternalInput")
    bada = k.dram("bada", [NL, 128, 48], F32, "ExternalInput")
    ng = k.dram("ng", [NL, 128, KC], F32, "ExternalInput")
    win = k.dram("win", [NL, N_IN // 128, 128, KC, 128], F32, "ExternalInput")
    consts = k.dram("consts", [NCONST, 128, 128], F32, "ExternalInput")
    lbl = k.dram("lbl", [128, KC, DEPTH], F32, "ExternalInput")
    lmask = k.dram("lmask", [128, KC, DEPTH], F32, "ExternalInput")
    hng = k.dram("hng", [NL, 128, 1], F32, "ExternalInput")
    out_t = k.dram("out", [D, SEQ], F32, "ExternalOutput")
    XT = k.dram("xo", [D, NT], F32, "ExternalOutput")
    Y = k.dram("Ys", [3, D, NT], BF16, "ExternalOutput" if "Y" in dbg else "Internal")
    dbg_t = {n: k.dram("dbg_" + n, list(s), d, "ExternalOutput") for n, (s, d) in ((a_, b_) for a_, b_ in dbg.items() if a_ not in ("Y", "_test"))}

    ones_f = k.sb("ones_f", [128, 128], F32)
    k.I("dve", "memset", ones_f[:], 1.0)
    cst = k.sb("cst", [128, NCONST, 128], F32)
    k.dma("sp", cst[:], consts.a.r("c p n -> p c n"))
    ident = cst[:, 0, :]
    eps_t = k.sb("eps_t", [128, 2], F32)
    k.I("dve", "memset", eps_t[:, 0:1], 1e-6)
    k.I("dve", "memset", eps_t[:, 1:2], 1e-5)

    hT = k.sb("hT", [128, KC, NT], BF16)
    sct = k.sb("sct", [128, KC, 2], F32)
    mods = k.sb("mods", [128, 48, 2], F32)
    sc1 = k.sb("sc1", [128, KC, 2], F32)
    bada_s = k.sb("bada_s", [128, 48], F32)
    ng_s = k.sb("ng_s", [128, KC], F32)
    wbuf = [k.sb("wb%d" % i, [128, KC, 128], BF16) for i in range(2)]
    psb = [k.ps("ps%d" % i, [128, 512], F32) for i in range(8)]
    st = {"ps": 0, "wb": 0}

    def next_ps():
        st["ps"] += 1
        return psb[st["ps"] % 8]

    def xt_view(t):
        return t.a.r("(kc p) n -> p kc n", p=128)

    lb_s = k.sb("lb_s", [128, KC, 1], F32)
    oml_s = k.sb("oml_s", [128, KC, 1], F32)
    with k.scope():
        le = k.sb("le", [128, KC, DEPTH], F32)
        lm = k.sb("lm", [128, KC], F32)
        k.dma("sp", le[:], lbl.a)
        k.I("dve", "tensor_reduce", lm[:], le[:], AX.X, ALU.max)
        k.I("dve", "tensor_tensor", le[:], le[:], lm[:].us(2).bc([128, KC, DEPTH]), ALU.subtract)
        k.I("act", "activation", le[:], le[:], AF.Exp)
        k.I("dve", "tensor_reduce", lm[:], le[:], AX.X, ALU.add)
        k.I("dve", "reciprocal", lm[:], lm[:])
        k.I("dve", "tensor_tensor", le[:], le[:], lm[:].us(2).bc([128, KC, DEPTH]), ALU.mult)
        lmk = k.sb("lmk", [128, KC, DEPTH], F32)
        k.dma("sp", lmk[:], lmask.a)
        k.I("dve", "tensor_tensor", le[:], le[:], lmk[:], ALU.mult)
        k.I("dve", "tensor_reduce", lb_s[:, :, 0], le[:], AX.X, ALU.add)
        k.I("dve", "tensor_scalar", oml_s[:], lb_s[:], -1.0, 1.0, ALU.mult, ALU.add)

    k.dma("sp", sct[:], ct_in.a)
    k.I("act", "activation", sct[:], sct[:], AF.Silu)

    def layer_pre(l, first):
        src = xt_in if first else XT
        k.dma("sp", bada_s[:], bada.a[l])
        k.dma("sp", ng_s[:], ng.a[l])
        with k.scope():
            wa_buf = [k.sb("wa%d" % i, [128, KC, 512], F32) for i in range(2)]
            pa = next_ps()
            for t12 in range(12):
                wt = wa_buf[t12 % 2]
                k.dma("sp", wt[:], wada.a[l, t12])
                for j4 in range(4):
                    j = t12 * 4 + j4
                    for kc in range(KC):
                        k.I("pe", "matmul", pa[:, 2 * j:2 * j + 2], wt[:, kc, j4 * 128:(j4 + 1) * 128], sct[:, kc, :],
                            start=(kc == 0), stop=(kc == KC - 1))
            k.I("dve", "tensor_tensor", mods[:], pa[:, 0:96].r("p (j c) -> p j c", c=2),
                bada_s[:].us(2).bc([128, 48, 2]), ALU.add)
            k.I("dve", "scalar_tensor_tensor", sc1[:], mods[:, 16:32, :], 1.0, ng_s[:].us(2).bc([128, KC, 2]),
                ALU.add, ALU.mult)
        with k.scope():
            x_ = k.sb("xb", [128, KC, 512], F32)
            sqb = k.sb("sqb", [128, KC, 512], F32)
            rstd = k.sb("rstd", [128, 512], F32)
            tmpb = k.sb("tmpb", [128, 512], F32)
            for (t0, tn) in TBLK:
                col = 1 if t0 == 0 else 0
                k.dma("sp", x_[:, :, 0:tn], xt_view(src)[:, :, t0:t0 + tn])
                k.I("act", "activation", sqb[:, :, 0:tn], x_[:, :, 0:tn], AF.Square)
                pss = next_ps()
                for kc in range(KC):
                    k.I("pe", "matmul", pss[:, 0:tn], ones_f[:], sqb[:, kc, 0:tn], start=(kc == 0), stop=(kc == KC - 1))
                k.I("act", "activation", rstd[:, 0:tn], pss[:, 0:tn], AF.Sqrt, bias=eps_t[:, 0:1], scale=1.0 / D)
                k.I("dve", "reciprocal", rstd[:, 0:tn], rstd[:, 0:tn])
                for kc in range(KC):
                    k.I("dve", "scalar_tensor_tensor", tmpb[:, 0:tn], x_[:, kc, 0:tn], sc1[:, kc, col:col + 1],
                        rstd[:, 0:tn], ALU.mult, ALU.mult)
                    k.I("act", "activation", hT[:, kc, t0:t0 + tn], tmpb[:, 0:tn], AF.Identity,
                        bias=mods[:, kc, col:col + 1], scale=1.0)
                if first:
                    k.dma("sp", xt_view(XT)[:, :, t0:t0 + tn], x_[:, :, 0:tn])

    def load_w(l, ct, ncols=128):
        wt = wbuf[st["wb"] % len(wbuf)]
        st["wb"] += 1
        k.dma("pool", wt[:, :, 0:ncols], win.a[l, ct][:, :, 0:ncols])
        return wt

    def proj_tile(l, ct, evac, ncols=128):
        wt = load_w(l, ct, ncols)
        for (t0, tn) in TBLK:
            p_ = next_ps()
            for kc in range(KC):
                k.I("pe", "matmul", p_[0:ncols, 0:tn], wt[:, kc, 0:ncols], hT[:, kc, t0:t0 + tn],
                    start=(kc == 0), stop=(kc == KC - 1))
            evac(p_[0:ncols, 0:tn], t0, tn)

    def proj_tok(l, ct, evac):
        wt = load_w(l, ct)
        for tt in range(NT // 128):
            p_ = next_ps()
            for kc in range(KC):
                k.I("pe", "matmul", p_[:, 0:128], hT[:, kc, tt * 128:(tt + 1) * 128], wt[:, kc, :],
                    start=(kc == 0), stop=(kc == KC - 1))
            evac(p_[:, 0:128], tt)

    NCH = NT // 64

    def hgrn_head(l, hd):
        base = N_A // 128
        with k.scope():
            qT = k.sb("hq", [128, NT], F32)
            sg = k.sb("hsg", [128, NT], F32)
            f = k.sb("hf", [128, NT], F32)
            tA = k.sb("hA", [128, NT], F32)
            tB = k.sb("hB", [128, NT], F32)
            tC = k.sb("hC", [128, NT], F32)
            tD = k.sb("hD", [128, NT], F32)
            X1 = k.sb("hX1", [128, NT], F32)
            X2 = k.sb("hX2", [128, NT], F32)
            oT = k.sb("ho", [128, NT], F32)
            vtok = k.sb("hv", [128, NT // 128, 128], F32)
            eb = k.sb("heb", [128, NCH], F32)
            hng_s = k.sb("hng_s", [128, 1], F32)
            Sb = [k.sb("hS%d" % i, [128, 128], F32) for i in range(2)]
            attb = [k.sb("hat%d" % i, [128, 128], F32) for i in range(2)]
            att2 = k.sb("hat2", [128, 128], F32)
            kbtb = [k.sb("hkb%d" % i, [128, 128], F32) for i in range(2)]
            ybf = [k.sb("hy%d" % i, [128, 512], BF16) for i in range(2)]
            tmp5 = k.sb("ht5", [128, 512], F32)
            rs5 = k.sb("hr5", [128, 512], F32)
            k.dma("sp", hng_s[:], hng.a[l])
            proj_tile(l, base + hd, lambda p, t0, tn: k.I("act", "activation", qT[:, t0:t0 + tn], p, AF.Silu))
            proj_tile(l, base + 64 + hd, lambda p, t0, tn: k.I("act", "activation", sg[:, t0:t0 + tn], p, AF.Silu))
            proj_tok(l, base + 48 + hd, lambda p, tt: k.I("dve", "tensor_copy", vtok[:, tt, :], p))
            lbp = lb_s[:, hd, 0:1]
            omp = oml_s[:, hd, 0:1]
            NS = NT // 32

            def v3(t_):
                return t_[:].r("p (c t) -> p c t", t=32)

            def v4(t_):
                return t_[:].r("p (c a t) -> p c a t", a=2, t=32)
            for j in range(2):
                proj_tile(l, base + 16 * (1 + j) + hd,
                          lambda p, t0, tn: k.I("act", "activation", f[:, t0:t0 + tn], p, AF.Sigmoid))
                k.I("dve", "tensor_scalar", f[:], f[:], omp, lbp, ALU.mult, ALU.add)
                k.I("act", "activation", tA[:], f[:], AF.Ln)
                k.I("dve", "tensor_scalar", f[:], f[:], -1.0, 1.0, ALU.mult, ALU.add)
                k.I("dve", "tensor_tensor_scan", tB[:], ones_f[:, 0:1].bc([128, NT]), tA[:], 0.0, ALU.mult, ALU.add)
                G3, g3, lc3 = v3(tB), v3(tA), v3(tC)
                lc4, bc4 = v4(tC), v4(tD)
                if j == 0:
                    k.I("dve", "tensor_copy", lc3[:, 0:1, :], G3[:, 0:1, :])
                    k.I("dve", "tensor_tensor", lc3[:, 1:NS, :], G3[:, 1:NS, :],
                        G3[:, 0:NS - 1, 31:32].bc([128, NS - 1, 32]), ALU.subtract)
                    first, second, last = 0, 1, 31
                else:
                    k.I("dve", "tensor_tensor", g3, g3, G3, ALU.subtract)
                    k.I("dve", "tensor_tensor", lc3, g3, G3[:, :, 31:32].bc([128, NS, 32]), ALU.add)
                    first, second, last = 1, 0, 0
                k.I("dve", "tensor_copy", bc4[:, :, first, :], lc4[:, :, first, :])
                k.I("dve", "tensor_tensor", bc4[:, :, second, :], lc4[:, :, second, :],
                    lc4[:, :, first, last:last + 1].bc([128, NCH, 32]), ALU.add)
                k.I("act", "activation", eb[:], bc4[:, :, second, last], AF.Exp)
                k.I("act", "activation", tA[:], tC[:], AF.Exp)
                k.I("dve", "tensor_tensor", X1[:], qT[:], tA[:], ALU.mult)
                k.I("act", "activation", tA[:], tC[:], AF.Exp, scale=-1.0)
                k.I("dve", "tensor_tensor", X2[:], f[:], tA[:], ALU.mult)
                k.I("dve", "tensor_tensor", v3(tB), lc3[:, :, last:last + 1].bc([128, NS, 32]), lc3, ALU.subtract)
                k.I("act", "activation", tB[:], tB[:], AF.Exp)
                k.I("dve", "tensor_tensor", tC[:], f[:], tB[:], ALU.mult)
                k.I("act", "activation", tA[:], tD[:], AF.Exp)
                k.I("dve", "tensor_tensor", tB[:], qT[:], tA[:], ALU.mult)
                bc3c = tD[:].r("p (c t) -> p c t", t=64)
                k.I("dve", "tensor_tensor", bc3c, bc4[:, :, second, last:last + 1].bc([128, NCH, 64]), bc3c, ALU.subtract)
                k.I("act", "activation", tD[:], tD[:], AF.Exp)
                k.I("dve", "tensor_tensor", f[:], f[:], tD[:], ALU.mult)
                qloc, kloc, kk2, qfull, kbar = X1, X2, tC, tB, f
                order = list(range(NT // 128)) if j == 0 else [1, 0] + list(range(NT // 128 - 1, 1, -1))
                cur = 0
                k.I("dve", "memset", Sb[0][:], 0.0)
                for it, tt in enumerate(order):
                    tok = slice(tt * 128, tt * 128 + 128)
                    pa = next_ps()
                    k.I("pe", "matmul", pa[:, 0:128], kloc[:, tok], qloc[:, tok], start=True, stop=True)
                    pa2 = next_ps()
                    k.I("pe", "matmul", pa2[:, 0:128], kk2[:, tok], qloc[:, tok], start=True, stop=True)
                    attT = attb[it % 2]
                    k.I("dve", "tensor_tensor", attT[:], pa[:, 0:128], cst[:, 6 + j, :], ALU.mult)
                    k.I("dve", "tensor_tensor", att2[:], pa2[:, 0:128], cst[:, 8 + j, :], ALU.mult)
                    k.I("dve", "tensor_tensor", attT[:], attT[:], att2[:], ALU.add)
                    pt = next_ps()
                    k.I("pe", "transpose", pt[:, 0:128], kbar[:, tok], ident)
                    kbt = kbtb[it % 2]
                    k.I("act", "copy", kbt[:], pt[:, 0:128])
                    po = next_ps()
                    k.I("pe", "matmul", po[:, 0:128], vtok[:, tt, :], attT[:], start=True, stop=False)
                    halves = [0, 1] if j == 0 else [1, 0]
                    for hi, hf in enumerate(halves):
                        c = tt * 2 + hf
                        k.I("pe", "matmul", po[:, hf * 64:hf * 64 + 64], Sb[cur][:],
                            qfull[:, tt * 128 + hf * 64:tt * 128 + hf * 64 + 64], start=False, stop=(hi == 1))
                        psu = next_ps()
                        k.I("pe", "matmul", psu[:, 0:128], kbt[hf * 64:hf * 64 + 64, :],
                            vtok[hf * 64:hf * 64 + 64, tt, :], start=True, stop=True)
                        k.I("dve", "scalar_tensor_tensor", Sb[1 - cur][:], Sb[cur][:], eb[:, c:c + 1], psu[:, 0:128],
                            ALU.mult, ALU.add)
                        cur = 1 - cur
                    if j == 0:
                        k.I("act", "copy", oT[:, tok], po[:, 0:128])
                    else:
                        k.I("dve", "tensor_tensor", oT[:, tok], oT[:, tok], po[:, 0:128], ALU.add)
            for ib, (t0, tn) in enumerate(TBLK):
                k.I("act", "activation", tmp5[:, 0:tn], oT[:, t0:t0 + tn], AF.Square)
                pss = next_ps()
                k.I("pe", "matmul", pss[:, 0:tn], ones_f[:], tmp5[:, 0:tn], start=True, stop=True)
                k.I("act", "activation", rs5[:, 0:tn], pss[:, 0:tn], AF.Sqrt, bias=eps_t[:, 1:2], scale=1.0 / 128)
                k.I("dve", "reciprocal", rs5[:, 0:tn], rs5[:, 0:tn])
                k.I("dve", "tensor_tensor", tmp5[:, 0:tn], oT[:, t0:t0 + tn], rs5[:, 0:tn], ALU.mult)
                yb_ = ybf[ib % 2]
                k.I("dve", "scalar_tensor_tensor", yb_[:, 0:tn], tmp5[:, 0:tn], hng_s[:, 0:1], sg[:, t0:t0 + tn],
                    ALU.mult, ALU.mult)
                k.dma("sp", Y.a[1, hd * 128:(hd + 1) * 128, t0:t0 + tn], yb_[:, 0:tn])

    rpar = k.dram("rpar", [NL, 128, 9, KC], F32, "ExternalInput")
    shw = k.dram("shw", [NL, 48, 128, 3], F32, "ExternalInput")
    shwl = k.dram("shwl", [NL, 4, 96, 3], F32, "ExternalInput")
    winl = k.dram("winl", [NL, 4, 128, KC, 96], F32, "ExternalInput")
    lup = k.dram("lup", [NL, 4, 96, D], F32, "ExternalInput")
    bdm = cst[:, 5, :]
    bdm3 = cst[:, 5, :].r("p (h s) -> p h s", h=2)

    def conv_shift(dst, praw, w3):
        k.I("dve", "tensor_scalar", dst, praw, w3[:, 1:2], None, ALU.mult)
        for (a, b_) in ((0, CTX), (CTX, NT)):
            k.I("dve", "scalar_tensor_tensor", dst[:, a + 1:b_], praw[:, a:b_ - 1], w3[:, 0:1], dst[:, a + 1:b_],
                ALU.mult, ALU.add)
            k.I("dve", "scalar_tensor_tensor", dst[:, a:b_ - 1], praw[:, a + 1:b_], w3[:, 2:3], dst[:, a:b_ - 1],
                ALU.mult, ALU.add)

    def rwkv_layer(l, pairs):
        with k.scope():
            rp = k.sb("rp", [128, 9, KC], F32)
            k.dma("sp", rp[:], rpar.a[l])
            lora = [k.sb("lora%d" % i, [96, NT], BF16) for i in range(4)]
            with k.scope():
                praw = k.sb("lpraw", [96, NT], F32)
                ltmp = k.sb("ltmp", [96, NT], F32)
                sw = k.sb("lsw", [96, 4, 3], F32)
                k.dma("sp", sw[:], shwl.a[l].r("j p t -> p j t"))
                for i in range(4):
                    wt = wbuf[st["wb"] % len(wbuf)]
                    st["wb"] += 1
                    k.dma("pool", wt[:, :, 0:96], winl.a[l, i])
                    for (t0, tn) in TBLK:
                        p_ = next_ps()
                        for kc in range(KC):
                            k.I("pe", "matmul", p_[0:96, 0:tn], wt[:, kc, 0:96], hT[:, kc, t0:t0 + tn],
                                start=(kc == 0), stop=(kc == KC - 1))
                        k.I("act", "copy", praw[:, t0:t0 + tn], p_[0:96, 0:tn])
                    conv_shift(ltmp[:], praw[:], sw[:, i, :])
                    if i < 2:
                        k.I("act", "activation", lora[i][:], ltmp[:], AF.Tanh)
                    else:
                        k.I("act", "copy", lora[i][:], ltmp[:])
            for hp in pairs:
                rwkv_pair(l, hp, rp, lora)

    def rwkv_pair(l, hp, rp, lora):
        par = lambda i: rp[:, i, hp:hp + 1]
        with k.scope():
            rT = k.sb("rr", [128, NT], F32)
            kT = k.sb("rk", [128, NT], F32)
            vT = k.sb("rv", [128, NT], F32)
            kkT = k.sb("rkk", [128, NT], F32)
            oT = k.sb("ro", [128, NT], F32)
            bnT = k.sb("rbn", [128, NT], BF16)
            sg = k.sb("rsg", [128, NT], BF16)
            lupt = k.sb("rlup", [96, 4, 128], BF16)
            sw = k.sb("rsw", [128, 3, 3], F32)
            k.dma("pool", lupt[:], lup.a[l, :, :, hp * 128:(hp + 1) * 128].r("j r c -> r j c"))
            for i in range(3):
                k.dma("sp", sw[:, i, :], shw.a[l, i * 16 + hp])

            def lora_act(dst, j, which, func):
                idx = which * 2 + j
                for (t0, tn) in TBLK:
                    p_ = next_ps()
                    k.I("pe", "matmul", p_[:, 0:tn], lupt[:, idx, :], lora[idx][:, t0:t0 + tn], start=True, stop=True)
                    k.I("act", "activation", dst[:, t0:t0 + tn], p_[:, 0:tn], func, bias=par(idx), scale=1.0)
            with k.scope():
                praw = k.sb("rpraw", [128, NT], F32)
                a0T = k.sb("ra0", [128, NT], F32)
                a1T = k.sb("ra1", [128, NT], F32)
                for i, dst in enumerate((rT, kT, vT)):
                    proj_tile(l, i * 16 + hp, lambda p, t0, tn: k.I("act", "copy", praw[:, t0:t0 + tn], p))
                    conv_shift(dst[:], praw[:], sw[:, i, :])
                proj_tile(l, N_SHIFT // 128 + hp, lambda p, t0, tn: k.I("act", "activation", sg[:, t0:t0 + tn], p, AF.Silu))
                lora_act(a0T, 0, 1, AF.Sigmoid)
                lora_act(a1T, 1, 1, AF.Sigmoid)
                k.I("dve", "tensor_scalar", kkT[:], kT[:], par(4), None, ALU.mult)
                k.I("act", "activation", praw[:], kkT[:], AF.Square)
                for (t0, tn) in TBLK:
                    p_ = next_ps()
                    k.I("pe", "matmul", p_[:, 0:tn], bdm, praw[:, t0:t0 + tn], start=True, stop=True)
                    k.I("act", "activation", a1T[:, t0:t0 + tn], p_[:, 0:tn], AF.Sqrt)
                k.I("dve", "tensor_scalar", praw[:], a1T[:], 1e-12, None, ALU.max)
                k.I("dve", "reciprocal", praw[:], praw[:])
                k.I("dve", "tensor_tensor", kkT[:], kkT[:], praw[:], ALU.mult)
                lora_act(a1T, 1, 1, AF.Sigmoid)
                k.I("dve", "tensor_tensor", a0T[:], a0T[:], a1T[:], ALU.add)
                k.I("dve", "tensor_scalar", a0T[:], a0T[:], -2.0, par(5), ALU.add, ALU.mult)
                k.I("dve", "scalar_tensor_tensor", a0T[:], a0T[:], 2.0, kT[:], ALU.add, ALU.mult)
                k.I("dve", "scalar_tensor_tensor", a0T[:], rT[:], par(6), a0T[:], ALU.mult, ALU.mult)
                for (t0, tn) in TBLK:
                    p_ = next_ps()
                    k.I("pe", "matmul", p_[:, 0:tn], bdm, a0T[:, t0:t0 + tn], start=True, stop=True)
                    k.I("dve", "tensor_tensor", bnT[:, t0:t0 + tn], p_[:, 0:tn], vT[:, t0:t0 + tn], ALU.mult)
            for j in range(2):
                with k.scope():
                    rwkv_dir(l, hp, j, par, lora_act, rT, kT, vT, kkT, oT)
            with k.scope():
                xc = k.sb("rxc", [128, 512], F32)
                sq = k.sb("rsq", [128, 512], F32)
                rs = k.sb("rrs", [128, 512], F32)
                ybf = [k.sb("ry%d" % i, [128, 512], BF16) for i in range(2)]
                geps = k.sb("rgeps", [128, 1], F32)
                k.I("dve", "memset", geps[:], 64e-5)
                for ib, (t0, tn) in enumerate(TBLK):
                    pm = next_ps()
                    k.I("pe", "matmul", pm[:, 0:tn], bdm, oT[:, t0:t0 + tn], start=True, stop=True)
                    k.I("dve", "scalar_tensor_tensor", xc[:, 0:tn], pm[:, 0:tn], -1.0 / 64, oT[:, t0:t0 + tn],
                        ALU.mult, ALU.add)
                    k.I("act", "activation", sq[:, 0:tn], xc[:, 0:tn], AF.Square)
                    pv = next_ps()
                    k.I("pe", "matmul", pv[:, 0:tn], bdm, sq[:, 0:tn], start=True, stop=True)
                    k.I("act", "activation", rs[:, 0:tn], pv[:, 0:tn], AF.Sqrt, bias=geps[:], scale=1.0 / 64)
                    k.I("dve", "reciprocal", rs[:, 0:tn], rs[:, 0:tn])
                    k.I("dve", "tensor_tensor", xc[:, 0:tn], xc[:, 0:tn], rs[:, 0:tn], ALU.mult)
                    k.I("dve", "tensor_scalar", xc[:, 0:tn], xc[:, 0:tn], par(7), par(8), ALU.mult, ALU.add)
                    k.I("dve", "tensor_tensor", xc[:, 0:tn], xc[:, 0:tn], bnT[:, t0:t0 + tn], ALU.add)
                    yb_ = ybf[ib % 2]
                    k.I("dve", "tensor_tensor", yb_[:, 0:tn], xc[:, 0:tn], sg[:, t0:t0 + tn], ALU.mult)
                    k.dma("sp", Y.a[0, hp * 128:(hp + 1) * 128, t0:t0 + tn], yb_[:, 0:tn])

    def rwkv_dir(l, hp, j, par, lora_act, rT, kT, vT, kkT, oT):
        aT = k.sb("ra", [128, NT], F32)
        lwT = k.sb("rlw", [128, NT], F32)
        lora_act(aT, j, 1, AF.Sigmoid)
        lora_act(lwT, j, 0, AF.Sigmoid)
        k.I("dve", "tensor_scalar", lwT[:], lwT[:], -0.6065306597126334, None, ALU.mult)
        NB = 1

        def buf(name, shape=(128, 128), n=NB):
            return [k.sb("%s%d" % (name, i), list(shape), F32) for i in range(n)]
        c64 = (128, 64)
        Lb, Lxb, E1b, E2b, E3b, E4b = (buf(n_, c64) for n_ in ("rL", "rLx", "rE1", "rE2", "rE3", "rE4"))
        totb = buf("rtot", (128, 2))
        t1b, kjb, bjb, Atb, Rtb = (buf(n_, c64) for n_ in ("rt1", "rkj", "rbj", "rAt", "rRt"))
        E2d, E3d, E4d, Abd, Khd, Bhd, Kbd, Bbd, Vfd = (buf(n_) for n_ in
                                                       ("rE2d", "rE3d", "rE4d", "rAbd", "rKhd", "rBhd", "rKbd", "rBbd", "rVfd"))
        Nmb, NmTb, Pb, Qb, QTb = (buf(n_) for n_ in ("rNm", "rNmT", "rP", "rQ", "rQT"))
        Mrk, Mrb = buf("rMrk", c64), buf("rMrb", c64)
        Makd = buf("rMak")
        Wst, Ust, Vst = buf("rW", c64), buf("rU", c64), buf("rVs", c64)
        Ubd, Vbd, Kt, Bt = buf("rUbd"), buf("rVbd"), buf("rKt"), buf("rBt")
        Hst = buf("rHs", c64, 2)
        Hbd = buf("rHbd", (128, 128), 2)
        for hb in Hbd:
            k.I("dve", "memset", hb[:], 0.0)
        for ub in Ubd:
            k.I("dve", "memset", ub[:], 0.0)
        k.I("dve", "memset", Hst[0][:], 0.0)
        hcur = 0
        if j == 0:
            chunks = list(range(NCH))
            ms = 3
            m_incl = cst[:, 10, 0:64]
        else:
            chunks = list(range(CTX // 64 - 1, -1, -1)) + list(range(NCH - 1, CTX // 64 - 1, -1))
            ms = 4
            m_incl = cst[:, 10, 64:128]
        m_str_bd = cst[:, ms, :]
        m_strT_bd = cst[:, 7 - ms, :]
        bd3 = lambda t_: t_[:].r("p (h s) -> p h s", h=2)
        bcx = lambda v_: v_.us(1).bc([128, 2, 64])
        for it, c in enumerate(chunks):
            b_ = it % NB
            tk = slice(c * 64, c * 64 + 64)
            L, Lx, E1, E2, E3, E4, tot = Lb[b_], Lxb[b_], E1b[b_], E2b[b_], E3b[b_], E4b[b_], totb[b_]
            k.I("dve", "tensor_tensor_scan", L[:], ones_f[:, 0:64], lwT[:, tk], 0.0, ALU.mult, ALU.add)
            k.I("dve", "tensor_copy", tot[:, 0:1], L[:, 63:64])
            if j == 1:
                k.I("dve", "tensor_tensor", L[:], lwT[:, tk], L[:], ALU.subtract)
                k.I("dve", "tensor_scalar", L[:], L[:], tot[:, 0:1], None, ALU.add)
            k.I("dve", "tensor_tensor", Lx[:], L[:], lwT[:, tk], ALU.subtract)
            k.I("act", "activation", E1[:], L[:], AF.Exp)
            k.I("act", "activation", E2[:], Lx[:], AF.Exp)
            k.I("act", "activation", E3[:], L[:], AF.Exp, scale=-1.0)
            k.I("act", "activation", E4[:], L[:], AF.Exp, bias=tot[:, 0:1], scale=-1.0)
            k.I("act", "activation", tot[:, 1:2], tot[:, 0:1], AF.Exp)
            for src, dst in ((E2, E2d[b_]), (E3, E3d[b_]), (E4, E4d[b_])):
                k.I("dve", "tensor_tensor", bd3(dst), bcx(src[:]), bdm3, ALU.mult)
            t1, kj, bj, At, Rt = t1b[b_], kjb[b_], bjb[b_], Atb[b_], Rtb[b_]
            k.I("dve", "tensor_scalar", t1[:], aT[:, tk], -1.0, par(5), ALU.add, ALU.mult)
            k.I("dve", "scalar_tensor_tensor", kj[:], t1[:], 1.0, kT[:, tk], ALU.add, ALU.mult)
            k.I("dve", "tensor_tensor", bj[:], kkT[:, tk], aT[:, tk], ALU.mult)
            k.I("dve", "tensor_tensor", At[:], kkT[:, tk], E2[:], ALU.mult)
            k.I("dve", "tensor_tensor", Rt[:], rT[:, tk], E1[:], ALU.mult)
            k.I("dve", "tensor_tensor", bd3(Abd[b_]), bcx(kkT[:, tk]), bd3(E2d[b_]), ALU.mult)
            k.I("dve", "tensor_tensor", bd3(Khd[b_]), bcx(kj[:]), bd3(E3d[b_]), ALU.mult)
            k.I("dve", "tensor_tensor", bd3(Bhd[b_]), bcx(bj[:]), bd3(E3d[b_]), ALU.mult)
            k.I("dve", "tensor_tensor", bd3(Kbd[b_]), bcx(kj[:]), bd3(E4d[b_]), ALU.mult)
            k.I("dve", "tensor_tensor", bd3(Bbd[b_]), bcx(bj[:]), bd3(E4d[b_]), ALU.mult)
            k.I("dve", "tensor_tensor", bd3(Vfd[b_]), bcx(vT[:, tk]), bdm3, ALU.mult)
            for src, dst in ((Kbd[b_], Kt[b_]), (Bbd[b_], Bt[b_]), (Vfd[b_], Vbd[b_])):
                pt = next_ps()
                k.I("pe", "transpose", pt[:, 0:128], src[:], ident)
                k.I("act", "copy", dst[:], pt[:, 0:128])
            for h2 in range(2):
                k.I("dve", "tensor_copy", Vst[b_][h2 * 64:(h2 + 1) * 64, :], Vbd[b_][h2 * 64:(h2 + 1) * 64, h2 * 64:(h2 + 1) * 64])
            pn = next_ps()
            k.I("pe", "matmul", pn[:, 0:128], Bhd[b_][:], Abd[b_][:], start=True, stop=True)
            pnt = next_ps()
            k.I("pe", "matmul", pnt[:, 0:128], Abd[b_][:], Bhd[b_][:], start=True, stop=True)
            Nm, NmT, P_, Q, QT = Nmb[b_], NmTb[b_], Pb[b_], Qb[b_], QTb[b_]
            k.I("dve", "tensor_tensor", Nm[:], pn[:, 0:128], m_str_bd, ALU.mult)
            k.I("dve", "tensor_tensor", NmT[:], pnt[:, 0:128], m_strT_bd, ALU.mult)
            k.I("dve", "tensor_tensor", P_[:], ident, Nm[:], ALU.subtract)
            X, XT = Nm, NmT
            for lev in range(5):
                pq = next_ps()
                k.I("pe", "matmul", pq[:, 0:128], XT[:], X[:], start=True, stop=True)
                pqt = next_ps()
                k.I("pe", "matmul", pqt[:, 0:128], X[:], XT[:], start=True, stop=True)
                if lev % 2 == 0:
                    nQ, nQT = Q, QT
                else:
                    nQ, nQT = Nm, NmT
                k.I("act", "copy", nQ[:], pq[:, 0:128])
                k.I("act", "copy", nQT[:], pqt[:, 0:128])
                X, XT = nQ, nQT
                pp = next_ps()
                k.I("pe", "matmul", pp[:, 0:128], XT[:], P_[:], start=True, stop=True)
                k.I("dve", "tensor_tensor", P_[:], P_[:], pp[:, 0:128], ALU.add)
            TT = P_
            p1 = next_ps()
            k.I("pe", "matmul", p1[:, 0:64], Khd[b_][:], Rt[:], start=True, stop=True)
            k.I("dve", "tensor_tensor", Mrk[b_][:], p1[:, 0:64], m_incl, ALU.mult)
            p2 = next_ps()
            k.I("pe", "matmul", p2[:, 0:64], Bhd[b_][:], Rt[:], start=True, stop=True)
            k.I("dve", "tensor_tensor", Mrb[b_][:], p2[:, 0:64], m_incl, ALU.mult)
            p3 = next_ps()
            k.I("pe", "matmul", p3[:, 0:128], Khd[b_][:], Abd[b_][:], start=True, stop=True)
            k.I("dve", "tensor_tensor", Makd[b_][:], p3[:, 0:128], m_str_bd, ALU.mult)
            pw = next_ps()
            k.I("pe", "matmul", pw[:, 0:64], Abd[b_][:], Hst[hcur][:], start=True, stop=False)
            k.I("pe", "matmul", pw[:, 0:64], Makd[b_][:], Vst[b_][:], start=False, stop=True)
            k.I("act", "copy", Wst[b_][:], pw[:, 0:64])
            pu = next_ps()
            k.I("pe", "matmul", pu[:, 0:64], TT[:], Wst[b_][:], start=True, stop=True)
            k.I("dve", "tensor_scalar", Ust[b_][:], pu[:, 0:64], -1.0, None, ALU.mult)
            for h2 in range(2):
                k.I("dve", "tensor_scalar", Ubd[b_][h2 * 64:(h2 + 1) * 64, h2 * 64:(h2 + 1) * 64],
                    pu[h2 * 64:(h2 + 1) * 64, 0:64], -1.0, None, ALU.mult)
            po = next_ps()
            k.I("pe", "matmul", po[:, 0:64], Hbd[hcur][:], Rt[:], start=True, stop=False)
            k.I("pe", "matmul", po[:, 0:64], Vbd[b_][:], Mrk[b_][:], start=False, stop=False)
            k.I("pe", "matmul", po[:, 0:64], Ubd[b_][:], Mrb[b_][:], start=False, stop=True)
            if j == 0:
                k.I("act", "copy", oT[:, tk], po[:, 0:64])
            else:
                k.I("dve", "tensor_tensor", oT[:, tk], oT[:, tk], po[:, 0:64], ALU.add)
            ph = next_ps()
            k.I("pe", "matmul", ph[:, 0:64], Kt[b_][:], Vst[b_][:], start=True, stop=False)
            k.I("pe", "matmul", ph[:, 0:64], Bt[b_][:], Ust[b_][:], start=False, stop=True)
            hn = 1 - hcur
            k.I("dve", "scalar_tensor_tensor", Hst[hn][:], Hst[hcur][:], tot[:, 1:2], ph[:, 0:64], ALU.mult, ALU.add)
            for h2 in range(2):
                k.I("act", "copy", Hbd[hn][h2 * 64:(h2 + 1) * 64, h2 * 64:(h2 + 1) * 64], Hst[hn][h2 * 64:(h2 + 1) * 64, :])
            hcur = hn

    nab = k.dram("nab", [NL, 16, 64, 16, 64], F32, "ExternalInput")
    Gs = k.dram("Gs", [3, D, NT], BF16, "Internal")
    wbr = k.dram("wbr", [NL, 3, KC, 128, KC, 128], F32, "ExternalInput")
    wo = k.dram("wo", [NL, KC, 128, KC, 128], F32, "ExternalInput")
    fg = k.dram("fg", [128, KC], F32, "ExternalInput")
    ones_b = k.sb("ones_b", [128, 128], BF16)
    k.I("dve", "memset", ones_b[:], 1.0)
    NA_SCALE = 128.0 ** -0.5
    NEG = -30000.0

    def na_types():
        def kps(R):
            lo = min(max(2 * R - 4, 0), 24) // 2
            hi = (min(max(2 * R + 1 - 4, 0), 24) + 7) // 2
            return list(range(lo, hi + 1))
        return kps

    def na_head(l, hd):
        base = (N_A + N_B) // 128
        kps = na_types()
        with k.scope():
            qT = k.sb("nq", [128, NT], BF16)
            kT = k.sb("nk", [128, NT], BF16)
            vtok = k.sb("nv", [128, NT // 128, 128], BF16)
            sg = k.sb("nsg", [128, NT], BF16)
            yT = k.sb("ny", [128, NT], BF16)
            blk2 = k.sb("nblk", [128, 16, 64], F32)
            btile = k.sb("nbt", [128, 21, 128], F32)
            tmpA = k.sb("ntA", [128, 512], F32)
            tmpB = k.sb("ntB", [128, 128], F32)
            PTA = [k.sb("nPA%d" % i, [128, 512], BF16) for i in range(2)]
            PTB = [k.sb("nPB%d" % i, [128, 384], BF16) for i in range(2)]
            rinv = k.sb("nri", [128, 256], F32)
            otmp = k.sb("not", [128, 256], F32)
            for half in range(2):
                k.dma("sp", blk2[half * 64:(half + 1) * 64, :, :], nab.a[l, hd])
            proj_tile(l, base + hd, lambda p, t0, tn: k.I("act", "mul", qT[:, t0:t0 + tn], p, NA_SCALE))
            proj_tile(l, base + 16 + hd, lambda p, t0, tn: k.I("act", "copy", kT[:, t0:t0 + tn], p))
            proj_tok(l, base + 32 + hd, lambda p, tt: k.I("dve", "tensor_copy", vtok[:, tt, :], p))
            proj_tile(l, base + 48 + hd, lambda p, t0, tn: k.I("act", "activation", sg[:, t0:t0 + tn], p, AF.Silu))
            toff = {}
            ti = 0
            for R in (0, 1, 7, 14, 15):
                toff[R] = ti
                for Kp in kps(R):
                    for ki in range(2):
                        for qi in range(2):
                            krow, qrow = 2 * Kp + ki, 2 * R + qi
                            rsq = min(max(qrow - 4, 0), 24)
                            idx = (krow - qrow + 7) if (rsq <= krow < rsq + 8) else 15
                            k.I("dve", "tensor_copy", btile[ki * 64:(ki + 1) * 64, ti, qi * 64:(qi + 1) * 64],
                                blk2[ki * 64:(ki + 1) * 64, idx, :])
                    ti += 1

            def finish(po, pr, t0, n):
                k.I("dve", "reciprocal", rinv[:, 0:n], pr[:, 0:n])
                k.I("dve", "tensor_tensor", otmp[:, 0:n], po[:, 0:n], rinv[:, 0:n], ALU.mult)
                k.I("dve", "tensor_tensor", yT[:, t0:t0 + n], otmp[:, 0:n], sg[:, t0:t0 + n], ALU.mult)
            psc = next_ps()
            for kc2 in range(2):
                k.I("pe", "matmul", psc[:, kc2 * 256:(kc2 + 1) * 256], kT[:, kc2 * 128:(kc2 + 1) * 128], qT[:, 0:256],
                    start=True, stop=True)
            k.I("act", "activation", PTA[0][:], psc[:, 0:512], AF.Exp)
            po, pr = next_ps(), next_ps()
            for kc2 in range(2):
                k.I("pe", "matmul", po[:, 0:256], vtok[:, kc2, :], PTA[0][:, kc2 * 256:(kc2 + 1) * 256],
                    start=(kc2 == 0), stop=(kc2 == 1))
            for kc2 in range(2):
                k.I("pe", "matmul", pr[:, 0:256], ones_b[:], PTA[0][:, kc2 * 256:(kc2 + 1) * 256],
                    start=(kc2 == 0), stop=(kc2 == 1))
            finish(po, pr, 0, 256)
            for R in range(16):
                Kl = kps(R)
                typ = R if R in (0, 1, 14, 15) else 7
                bo = toff[typ]
                qs = slice(256 + R * 128, 256 + R * 128 + 128)
                pA, pB = next_ps(), next_ps()
                nl = len(Kl)
                for i, Kp in enumerate(Kl):
                    dst = pA[:, i * 128:(i + 1) * 128] if i < 4 else pB[:, 0:128]
                    k.I("pe", "matmul", dst, kT[:, 256 + Kp * 128:256 + Kp * 128 + 128], qT[:, qs], start=True, stop=True)
                cb = 128 if nl == 5 else 0
                for kc2 in range(2):
                    k.I("pe", "matmul", pB[:, cb + kc2 * 128:cb + (kc2 + 1) * 128], kT[:, kc2 * 128:(kc2 + 1) * 128],
                        qT[:, qs], start=True, stop=True)
                PA, PB = PTA[R % 2], PTB[R % 2]
                k.I("dve", "tensor_tensor", tmpA[:], pA[:, 0:512], btile[:, bo:bo + 4, :].r("p a b -> p (a b)"), ALU.add)
                k.I("act", "activation", PA[:], tmpA[:], AF.Exp)
                if nl == 5:
                    k.I("dve", "tensor_tensor", tmpB[:], pB[:, 0:128], btile[:, bo + 4, :], ALU.add)
                    k.I("act", "activation", PB[:, 0:128], tmpB[:], AF.Exp)
                k.I("act", "activation", PB[:, cb:cb + 256], pB[:, cb:cb + 256], AF.Exp)
                srcs = []
                for i, Kp in enumerate(Kl):
                    srcs.append((2 + Kp, PA[:, i * 128:(i + 1) * 128] if i < 4 else PB[:, 0:128]))
                for kc2 in range(2):
                    srcs.append((kc2, PB[:, cb + kc2 * 128:cb + (kc2 + 1) * 128]))
                po, pr = next_ps(), next_ps()
                for i, (tt, pv) in enumerate(srcs):
                    k.I("pe", "matmul", po[:, 0:128], vtok[:, tt, :], pv, start=(i == 0), stop=(i == len(srcs) - 1))
                for i, (tt, pv) in enumerate(srcs):
                    k.I("pe", "matmul", pr[:, 0:128], ones_b[:], pv, start=(i == 0), stop=(i == len(srcs) - 1))
                finish(po, pr, 256 + R * 128, 128)
            k.dma("sp", Y.a[2, hd * 128:(hd + 1) * 128, :], yT[:])

    def gates(l):
        gt0 = (N_A + N_B + N_C) // 128
        with k.scope():
            gb = [k.sb("gb%d" % i, [128, NT], BF16) for i in range(2)]
            for i in range(48):
                g_ = gb[i % 2]
                proj_tile(l, gt0 + i, lambda p, t0, tn: k.I("act", "activation", g_[:, t0:t0 + tn], p, AF.Sigmoid))
                k.dma("sp", Gs.a[i // 16, (i % 16) * 128:(i % 16 + 1) * 128, :], g_[:])

    def merge(l, last):
        with k.scope():
            yblk = [k.sb("my%d" % i, [128, KC, 512], BF16) for i in range(3)]
            mT = k.sb("mm", [128, KC, 512], BF16)
            g3 = [k.sb("mg%d" % i, [128, 3, 512], BF16) for i in range(2)]
            wb_ = [k.sb("mw%d" % i, [128, KC, 128], BF16) for i in range(4)]
            xo = [k.sb("mx%d" % i, [128, 512], F32) for i in range(2)]
            xn = [k.sb("mxn%d" % i, [128, 512], F32) for i in range(2)]
            acc = k.sb("macc", [128, 512], F32)
            acc2 = k.sb("macc2", [128, 512], F32)
            wi = 0
            for (t0, tn) in TBLK:
                if last and t0 == 0:
                    continue
                col = 1 if t0 == 0 else 0
                for br in range(3):
                    k.dma("sp", yblk[br][:, :, 0:tn], Y.a[br].r("(kc p) n -> p kc n", p=128)[:, :, t0:t0 + tn])
                for oc in range(KC):
                    g_ = g3[oc % 2]
                    k.dma("sp", g_[:, :, 0:tn], Gs.a[:, oc * 128:(oc + 1) * 128, t0:t0 + tn].r("b p n -> p b n"))
                    pss = []
                    for br in range(3):
                        w_ = wb_[wi % 4]
                        wi += 1
                        k.dma("pool", w_[:], wbr.a[l, br, oc])
                        p_ = next_ps()
                        for kc in range(KC):
                            k.I("pe", "matmul", p_[:, 0:tn], w_[:, kc, :], yblk[br][:, kc, 0:tn],
                                start=(kc == 0), stop=(kc == KC - 1))
                        pss.append(p_)
                    k.I("dve", "tensor_tensor", acc[:, 0:tn], pss[0][:, 0:tn], g_[:, 0, 0:tn], ALU.mult)
                    k.I("dve", "tensor_tensor", acc2[:, 0:tn], pss[1][:, 0:tn], g_[:, 1, 0:tn], ALU.mult)
                    k.I("dve", "tensor_tensor", acc[:, 0:tn], acc[:, 0:tn], acc2[:, 0:tn], ALU.add)
                    k.I("dve", "tensor_tensor", acc2[:, 0:tn], pss[2][:, 0:tn], g_[:, 2, 0:tn], ALU.mult)
                    k.I("dve", "tensor_tensor", mT[:, oc, 0:tn], acc[:, 0:tn], acc2[:, 0:tn], ALU.add)
                for oc in range(KC):
                    w_ = wb_[wi % 4]
                    wi += 1
                    k.dma("pool", w_[:], wo.a[l, oc])
                    x_, xn_ = xo[oc % 2], xn[oc % 2]
                    k.dma("sp", x_[:, 0:tn], XT.a[oc * 128:(oc + 1) * 128, t0:t0 + tn])
                    p_ = next_ps()
                    for kc in range(KC):
                        k.I("pe", "matmul", p_[:, 0:tn], w_[:, kc, :], mT[:, kc, 0:tn], start=(kc == 0), stop=(kc == KC - 1))
                    k.I("dve", "scalar_tensor_tensor", xn_[:, 0:tn], p_[:, 0:tn], mods[:, 32 + oc, col:col + 1], x_[:, 0:tn],
                        ALU.mult, ALU.add)
                    k.dma("sp", XT.a[oc * 128:(oc + 1) * 128, t0:t0 + tn], xn_[:, 0:tn])

    def final_norm():
        with k.scope():
            fg_s = k.sb("fg_s", [128, KC], F32)
            k.dma("sp", fg_s[:], fg.a)
            x_ = k.sb("fx", [128, KC, 512], F32)
            sqb = k.sb("fsq", [128, KC, 512], F32)
            rstd = k.sb("frs", [128, 512], F32)
            for (t0, tn) in TBLK[1:]:
                k.dma("sp", x_[:], xt_view(XT)[:, :, t0:t0 + tn])
                k.I("act", "activation", sqb[:], x_[:], AF.Square)
                pss = next_ps()
                for kc in range(KC):
                    k.I("pe", "matmul", pss[:, 0:tn], ones_f[:], sqb[:, kc, :], start=(kc == 0), stop=(kc == KC - 1))
                k.I("act", "activation", rstd[:], pss[:, 0:tn], AF.Sqrt, bias=eps_t[:, 0:1], scale=1.0 / D)
                k.I("dve", "reciprocal", rstd[:], rstd[:])
                for kc in range(KC):
                    k.I("dve", "scalar_tensor_tensor", sqb[:, kc, :], x_[:, kc, :], fg_s[:, kc:kc + 1], rstd[:],
                        ALU.mult, ALU.mult)
                k.dma("sp", out_t.a.r("(kc p) n -> p kc n", p=128)[:, :, t0 - CTX:t0 - CTX + tn], sqb[:])

    tp = dbg.get("_test")
    if tp is not None:
        layer_pre(0, True)
        for hd in tp.get("hgrn", []):
            hgrn_head(0, hd)
        if tp.get("rwkv"):
            rwkv_layer(0, tp["rwkv"])
        for hd in tp.get("na", []):
            na_head(0, hd)
        with k.scope():
            zz = k.sb("zz", [128, SEQ], F32)
            k.I("dve", "memset", zz[:], 0.0)
            for kc in range(KC):
                k.dma("sp", out_t.a[kc * 128:(kc + 1) * 128, :], zz[:])
        return k.finish(), k
    l = 0
    layer_pre(l, True)
    gates(l)
    rwkv_layer(l, list(range(16)))
    for hd in range(16):
        hgrn_head(l, hd)
    for hd in range(16):
        na_head(l, hd)
    k.barrier()
    merge(l, False)
    k.barrier()
    final_norm()
    return k.finish(), k


def na_table(rpb):
    c = np.arange(64)
    cs = np.clip(c - 8, 0, 48)
    cp = np.arange(64)
    valid = (cp[:, None] >= cs[None, :]) & (cp[:, None] < cs[None, :] + 16)
    dc = np.clip(cp[:, None] - c[None, :] + 15, 0, 30)
    H = rpb.shape[0]
    out = np.full((H, 64, 16, 64), -30000.0, np.float32)
    for dr in range(15):
        g = rpb[:, dr, :][:, dc]
        out[:, :, dr, :] = np.where(valid[None], g, np.float32(-30000.0))
    return out


LAYERED = ("wada", "bada", "ng", "win", "rpar", "shw", "shwl", "winl", "lup", "hng", "nab", "wbr", "wo")


def layer_inputs(shared, l):
    d = {}
    for n, v in shared.items():
        d[n] = v[l:l + 1] if n in LAYERED else v
    lm = np.zeros((128, KC, DEPTH), np.float32)
    lm[:, :, 1:l + 1] = 1.0
    d["lmask"] = lm
    return d


def prep_inputs(inp, NL=DEPTH):
    f = np.float32
    x, c, ctx, c_ctx = (np.asarray(inp[n], f) for n in ("x", "c", "ctx", "c_ctx"))
    shared = {}
    w_ada = np.asarray(inp["w_ada"], f)[:NL]
    shared["wada"] = np.ascontiguousarray(w_ada.reshape(NL, KC, 128, 12, 512).transpose(0, 3, 2, 1, 4))
    shared["bada"] = np.ascontiguousarray(np.asarray(inp["b_ada"], f)[:NL].reshape(NL, 48, 128).transpose(0, 2, 1))
    shared["ng"] = np.stack([pc(v) for v in np.asarray(inp["norm_g"], f)[:NL]])
    w_in = np.asarray(inp["w_in"], f)[:NL]
    shared["win"] = np.ascontiguousarray(w_in.reshape(NL, KC, 128, N_IN // 128, 128).transpose(0, 3, 2, 1, 4))
    shared["consts"] = make_consts()
    shared["lbl"] = np.ascontiguousarray(np.asarray(inp["hgrn_lb_logits"], f).reshape(DEPTH, KC, 128).transpose(2, 1, 0))
    g_ = lambda n: np.asarray(inp[n], f)[:NL]
    rp = []
    for l in range(NL):
        vs = [g_("rwkv_w0")[l, 0], g_("rwkv_w0")[l, 1], g_("rwkv_a0")[l, 0], g_("rwkv_a0")[l, 1], g_("rwkv_k_k")[l],
              g_("rwkv_k_a")[l], g_("rwkv_r_k")[l].reshape(-1), g_("rwkv_ln_g")[l], g_("rwkv_ln_b")[l]]
        rp.append(np.stack([pc(v) for v in vs], axis=1))
    shared["rpar"] = np.ascontiguousarray(np.stack(rp))
    sh = g_("rwkv_shift")
    shared["shw"] = np.ascontiguousarray(sh[:, :, :6144].reshape(NL, 3, 48, 128).transpose(0, 2, 3, 1))
    shared["shwl"] = np.ascontiguousarray(sh[:, :, 6144:].reshape(NL, 3, 4, 96).transpose(0, 2, 3, 1))
    shared["winl"] = np.ascontiguousarray(w_in[:, :, 6144:6528].reshape(NL, KC, 128, 4, 96).transpose(0, 3, 2, 1, 4))
    shared["lup"] = np.ascontiguousarray(np.concatenate([g_("rwkv_w_up"), g_("rwkv_a_up")], axis=1))
    shared["hng"] = np.ascontiguousarray(np.asarray(inp["hgrn_norm_g"], f)[:NL].reshape(NL, 128, 1))
    shared["nab"] = np.stack([na_table(r_) for r_ in g_("na_rpb")])
    shared["wbr"] = np.ascontiguousarray(g_("w_branch").reshape(NL, 3, KC, 128, KC, 128).transpose(0, 1, 4, 3, 2, 5))
    shared["wo"] = np.ascontiguousarray(g_("w_out").reshape(NL, KC, 128, KC, 128).transpose(0, 3, 2, 1, 4))
    shared["fg"] = pc(np.asarray(inp["final_g"], f))
    xts = [np.ascontiguousarray(np.concatenate([ctx[b], x[b]], axis=0).T) for b in range(B)]
    cts = [np.ascontiguousarray(np.stack([pc(c[b]), pc(c_ctx)], axis=-1)) for b in range(B)]
    return shared, xts, cts


def kernel(**inputs):
    shared, xts, cts = prep_inputs(inputs)
    nc, _ = build()
    res = None
    for l in range(DEPTH):
        li = layer_inputs(shared, l)
        in_maps = []
        for i in range(B):
            d = dict(li)
            d["xt"] = xts[i]
            d["ct"] = cts[i]
            in_maps.append(d)
        res = run_bass_kernel_spmd(nc, in_maps, core_ids=list(range(B)))
        xts = [np.asarray(res.results[b]["xo"]) for b in range(B)]
    out = np.stack([np.ascontiguousarray(np.asarray(res.results[b]["out"]).T) for b in range(B)])
    return out.astype(np.float32)
```
